# Optimizing a Trainium2 kernel written in Bass

```python
import math
import jax, jax.numpy as jnp
from jax import lax
import numpy as np

D_MODEL = 2048
BATCH = 4
SEQ = 2048
DEPTH = 1
DEC_BATCH = 128
DEC_SEQ = 1
PAST_LEN = 16384
PAGE_SIZE = 128

D_MIX = D_MODEL
D_RWKV = D_MIX // 2
D_CONV = D_MIX - D_RWKV
HEAD_DIM = 64
N_HEADS_RWKV = D_RWKV // HEAD_DIM
CONV_GROUP = 64
N_CONV_GROUPS = D_CONV // CONV_GROUP
DECAY_LORA = 64
ICLR_LORA = 64
GATE_LORA = 160
CONV_WIDTH = 31
N_MEM = 256
N_MEM_HEADS = 4
MEM_HEAD_DIM = D_MODEL // N_MEM_HEADS
D_FF = 5632
ALPHA = (2.0 * DEPTH) ** 0.25
BETA = (8.0 * DEPTH) ** -0.25
LN_EPS = 1e-5
GN_EPS = 64e-5
SHIFT_COLS = 3 * D_RWKV + DECAY_LORA + ICLR_LORA + GATE_LORA
IN_COLS = SHIFT_COLS + 2 * D_CONV
SPLITS = [D_RWKV, 2 * D_RWKV, 3 * D_RWKV, 3 * D_RWKV + DECAY_LORA, 3 * D_RWKV + DECAY_LORA + ICLR_LORA]

kernel_name = "hymba_rwkv7_conformer_macaron_deepnorm_step"


def layer_norm(x, g, b, eps=LN_EPS):
    xf = x.astype(jnp.float32)
    mu = jnp.mean(xf, axis=-1, keepdims=True)
    var = jnp.mean(jnp.square(xf - mu), axis=-1, keepdims=True)
    return ((xf - mu) * lax.rsqrt(var + eps) * g.astype(jnp.float32) + b.astype(jnp.float32)).astype(x.dtype)


def swiglu(x, w1, w3, w2):
    return (jax.nn.silu(x @ w1) * (x @ w3)) @ w2


def rwkv7_recurrence(state, r, decay, k, v, kk, a):
    def step(S, inp):
        r_t, w_t, k_t, v_t, kk_t, a_t = inp
        sa = jnp.einsum('bhvk,bhk->bhv', S, -kk_t)
        S = (S * w_t[:, :, None, :]
             + sa[..., None] * (kk_t * a_t)[:, :, None, :]
             + v_t[..., None] * k_t[:, :, None, :])
        y_t = jnp.einsum('bhvk,bhk->bhv', S, r_t)
        return S, y_t
    xs = tuple(jnp.swapaxes(t, 0, 1) for t in (r, decay, k, v, kk, a))
    S, ys = lax.scan(step, state, xs)
    return jnp.swapaxes(ys, 0, 1), S


def parallel_mixer(h, shift_prev, conv_prev, wkv_prev, p):
    B, T, _ = h.shape
    dt = h.dtype
    f32 = jnp.float32
    proj = h @ p['w_in']
    ps, pc = proj[..., :SHIFT_COLS], proj[..., SHIFT_COLS:]
    ps_prev = jnp.concatenate([shift_prev[:, None, :].astype(dt), ps[:, :-1]], axis=1)
    pm = ps + p['mu_shift'] * (ps_prev - ps)
    r, k, v, wd, ad, gd = jnp.split(pm, SPLITS, axis=-1)

    w_log = -jax.nn.softplus(-(p['w0'] + jnp.tanh(wd) @ p['w2_decay']).astype(f32)) - 0.5
    decay = jnp.exp(-jnp.exp(w_log))
    a = jax.nn.sigmoid((p['a0'] + ad @ p['a2_iclr']).astype(f32))
    g = (jax.nn.sigmoid(gd) @ p['g2_gate']).astype(f32)
    rf, kf, vf = r.astype(f32), k.astype(f32), v.astype(f32)
    kk = (kf * p['k_k'].astype(f32)).reshape(B, T, N_HEADS_RWKV, HEAD_DIM)
    kk = kk / jnp.maximum(jnp.linalg.norm(kk, axis=-1, keepdims=True), 1e-12)
    kf = kf * (1.0 + (a - 1.0) * p['k_a'].astype(f32))
    hd = lambda t: t.reshape(B, T, N_HEADS_RWKV, HEAD_DIM)
    rh, kh, vh, wh, ah = hd(rf), hd(kf), hd(vf), hd(decay), hd(a)
    y, wkv_new = rwkv7_recurrence(wkv_prev.astype(f32), rh, wh, kh, vh, kk, ah)
    mu = jnp.mean(y, axis=-1, keepdims=True)
    var = jnp.mean(jnp.square(y - mu), axis=-1, keepdims=True)
    yn = ((y - mu) * lax.rsqrt(var + GN_EPS)).reshape(B, T, D_RWKV)
    yn = yn * p['gn_g'].astype(f32) + p['gn_b'].astype(f32)
    bonus = (jnp.sum(rh * kh * p['r_k'].astype(f32), axis=-1, keepdims=True) * vh).reshape(B, T, D_RWKV)
    o_a = ((yn + bonus) * g).astype(dt)

    u = pc[..., :D_CONV] * jax.nn.sigmoid(pc[..., D_CONV:])
    ubuf = jnp.concatenate([conv_prev.astype(dt), u], axis=1)
    c = lax.conv_general_dilated(ubuf, p['conv_w'], window_strides=(1,), padding='VALID',
                                 dimension_numbers=('NWC', 'WIO', 'NWC'),
                                 feature_group_count=D_CONV) + p['conv_b']
    o_b = jax.nn.silu(layer_norm(c, p['conv_ln_g'], p['conv_ln_b']))

    merged = jnp.concatenate([o_a * p['beta_rwkv'], o_b * p['beta_conv']], axis=-1) @ p['w_out']
    return merged, ps[:, -1], ubuf[:, -(CONV_WIDTH - 1):], wkv_new


def mem_kv(mem, w_mk, w_mv):
    B = mem.shape[0]
    mk = (mem @ w_mk).reshape(B, N_MEM, N_MEM_HEADS, MEM_HEAD_DIM)
    mv = (mem @ w_mv).reshape(B, N_MEM, N_MEM_HEADS, MEM_HEAD_DIM)
    return mk, mv


def mem_attend(h, mk, mv, w_mq, w_mo):
    B, T, _ = h.shape
    q = (h @ w_mq).reshape(B, T, N_MEM_HEADS, MEM_HEAD_DIM)
    s = jnp.einsum('bthd,bshd->bhts', q, mk.astype(h.dtype)).astype(jnp.float32) / math.sqrt(MEM_HEAD_DIM)
    pr = jax.nn.softmax(s, axis=-1).astype(h.dtype)
    o = jnp.einsum('bhts,bshd->bthd', pr, mv.astype(h.dtype)).reshape(B, T, D_MODEL)
    return o @ w_mo


def trunk_layer(x, shift_prev, conv_prev, wkv_prev, mk, mv, p):
    x = layer_norm(ALPHA * x + 0.5 * swiglu(x, p['ffn1_w1'], p['ffn1_w3'], p['ffn1_w2']), p['ln1_g'], p['ln1_b'])
    mix, sh, cv, wkv = parallel_mixer(x, shift_prev, conv_prev, wkv_prev, p)
    x = layer_norm(ALPHA * x + mix, p['ln2_g'], p['ln2_b'])
    x = layer_norm(ALPHA * x + mem_attend(x, mk, mv, p['w_mq'], p['w_mo']), p['ln3_g'], p['ln3_b'])
    x = layer_norm(ALPHA * x + 0.5 * swiglu(x, p['ffn2_w1'], p['ffn2_w3'], p['ffn2_w2']), p['ln4_g'], p['ln4_b'])
    return x, sh, cv, wkv


def setup_inputs(seed: int = 0) -> dict:
    key = jax.random.key(seed)
    ks = iter(jax.random.split(key, 64))
    nrm = lambda shape, s=1.0: jax.random.normal(next(ks), shape, jnp.float32) * s
    gain = lambda n: 1.0 + nrm((DEPTH, n), 0.02)
    bias = lambda n: nrm((DEPTH, n), 0.02)
    L = DEPTH
    d = {}
    d['x_prompt'] = nrm((BATCH, SEQ, D_MODEL))
    d['x_sample'] = nrm((DEC_BATCH, DEC_SEQ, D_MODEL))
    d['mem_prompt'] = nrm((BATCH, N_MEM, D_MODEL))
    d['state_shift'] = nrm((L, DEC_BATCH, SHIFT_COLS))
    d['state_conv'] = nrm((L, DEC_BATCH, CONV_WIDTH - 1, D_CONV), 0.5)
    d['state_wkv'] = nrm((L, DEC_BATCH, N_HEADS_RWKV, HEAD_DIM, HEAD_DIM), 0.5)
    d['cache_mem_k'] = nrm((L, DEC_BATCH, N_MEM, N_MEM_HEADS, MEM_HEAD_DIM))
    d['cache_mem_v'] = nrm((L, DEC_BATCH, N_MEM, N_MEM_HEADS, MEM_HEAD_DIM))
    d['ffn1_w1'] = nrm((L, D_MODEL, D_FF), D_MODEL ** -0.5)
    d['ffn1_w3'] = nrm((L, D_MODEL, D_FF), D_MODEL ** -0.5)
    d['ffn1_w2'] = nrm((L, D_FF, D_MODEL), BETA * D_FF ** -0.5)
    d['ln1_g'] = gain(D_MODEL)
    d['ln1_b'] = bias(D_MODEL)
    d['w_in'] = nrm((L, D_MODEL, IN_COLS), D_MODEL ** -0.5)
    d['mu_shift'] = jax.random.uniform(next(ks), (L, SHIFT_COLS), jnp.float32)
    d['w0'] = jax.random.uniform(next(ks), (L, D_RWKV), jnp.float32, -6.0, -0.5)
    d['w2_decay'] = nrm((L, DECAY_LORA, D_RWKV), 0.1 * DECAY_LORA ** -0.5)
    d['a0'] = nrm((L, D_RWKV), 0.1)
    d['a2_iclr'] = nrm((L, ICLR_LORA, D_RWKV), 0.5 * ICLR_LORA ** -0.5)
    d['g2_gate'] = nrm((L, GATE_LORA, D_RWKV), GATE_LORA ** -0.5)
    d['k_k'] = 0.85 + nrm((L, D_RWKV), 0.05)
    d['k_a'] = 1.0 + nrm((L, D_RWKV), 0.05)
    d['r_k'] = nrm((L, N_HEADS_RWKV, HEAD_DIM), 0.1)
    d['gn_g'] = gain(D_RWKV)
    d['gn_b'] = bias(D_RWKV)
    d['conv_w'] = nrm((L, CONV_WIDTH, 1, D_CONV), CONV_WIDTH ** -0.5)
    d['conv_b'] = bias(D_CONV)
    d['conv_ln_g'] = gain(D_CONV)
    d['conv_ln_b'] = bias(D_CONV)
    d['beta_rwkv'] = gain(D_RWKV)
    d['beta_conv'] = gain(D_CONV)
    d['w_out'] = nrm((L, D_MIX, D_MODEL), BETA * D_MIX ** -0.5)
    d['ln2_g'] = gain(D_MODEL)
    d['ln2_b'] = bias(D_MODEL)
    d['w_mq'] = nrm((L, D_MODEL, D_MODEL), D_MODEL ** -0.5)
    d['w_mk'] = nrm((L, D_MODEL, D_MODEL), D_MODEL ** -0.5)
    d['w_mv'] = nrm((L, D_MODEL, D_MODEL), D_MODEL ** -0.5)
    d['w_mo'] = nrm((L, D_MODEL, D_MODEL), BETA * D_MODEL ** -0.5)
    d['ln3_g'] = gain(D_MODEL)
    d['ln3_b'] = bias(D_MODEL)
    d['ffn2_w1'] = nrm((L, D_MODEL, D_FF), D_MODEL ** -0.5)
    d['ffn2_w3'] = nrm((L, D_MODEL, D_FF), D_MODEL ** -0.5)
    d['ffn2_w2'] = nrm((L, D_FF, D_MODEL), BETA * D_FF ** -0.5)
    d['ln4_g'] = gain(D_MODEL)
    d['ln4_b'] = bias(D_MODEL)
    return d


def reference(x_prompt, x_sample, mem_prompt, state_shift, state_conv, state_wkv, cache_mem_k, cache_mem_v,
              ffn1_w1, ffn1_w3, ffn1_w2, ln1_g, ln1_b, w_in, mu_shift, w0, w2_decay, a0, a2_iclr, g2_gate,
              k_k, k_a, r_k, gn_g, gn_b, conv_w, conv_b, conv_ln_g, conv_ln_b, beta_rwkv, beta_conv, w_out,
              ln2_g, ln2_b, w_mq, w_mk, w_mv, w_mo, ln3_g, ln3_b, ffn2_w1, ffn2_w3, ffn2_w2, ln4_g, ln4_b):
    B = x_prompt.shape[0]
    dt = x_prompt.dtype
    hp, hs = x_prompt, x_sample
    sh_p_all, cv_p_all, wkv_p_all, mk_p_all, mv_p_all = [], [], [], [], []
    sh_s_all, cv_s_all, wkv_s_all = [], [], []
    for l in range(DEPTH):
        p = dict(ffn1_w1=ffn1_w1[l], ffn1_w3=ffn1_w3[l], ffn1_w2=ffn1_w2[l], ln1_g=ln1_g[l], ln1_b=ln1_b[l],
                 w_in=w_in[l], mu_shift=mu_shift[l], w0=w0[l], w2_decay=w2_decay[l], a0=a0[l],
                 a2_iclr=a2_iclr[l], g2_gate=g2_gate[l], k_k=k_k[l], k_a=k_a[l], r_k=r_k[l],
                 gn_g=gn_g[l], gn_b=gn_b[l], conv_w=conv_w[l], conv_b=conv_b[l], conv_ln_g=conv_ln_g[l],
                 conv_ln_b=conv_ln_b[l], beta_rwkv=beta_rwkv[l], beta_conv=beta_conv[l], w_out=w_out[l],
                 ln2_g=ln2_g[l], ln2_b=ln2_b[l], w_mq=w_mq[l], w_mo=w_mo[l], ln3_g=ln3_g[l], ln3_b=ln3_b[l],
                 ffn2_w1=ffn2_w1[l], ffn2_w3=ffn2_w3[l], ffn2_w2=ffn2_w2[l], ln4_g=ln4_g[l], ln4_b=ln4_b[l])
        mk_p, mv_p = mem_kv(mem_prompt, w_mk[l], w_mv[l])
        hp, sh_p, cv_p, wkv_p = trunk_layer(
            hp, jnp.zeros((B, SHIFT_COLS), dt), jnp.zeros((B, CONV_WIDTH - 1, D_CONV), dt),
            jnp.zeros((B, N_HEADS_RWKV, HEAD_DIM, HEAD_DIM), jnp.float32), mk_p, mv_p, p)
        hs, sh_s, cv_s, wkv_s = trunk_layer(
            hs, state_shift[l], state_conv[l], state_wkv[l], cache_mem_k[l], cache_mem_v[l], p)
        sh_p_all.append(sh_p); cv_p_all.append(cv_p); wkv_p_all.append(wkv_p)
        mk_p_all.append(mk_p); mv_p_all.append(mv_p)
        sh_s_all.append(sh_s); cv_s_all.append(cv_s); wkv_s_all.append(wkv_s)
    return (hp, hs,
            jnp.stack(sh_p_all), jnp.stack(cv_p_all), jnp.stack(wkv_p_all),
            jnp.stack(mk_p_all), jnp.stack(mv_p_all),
            jnp.stack(sh_s_all), jnp.stack(cv_s_all), jnp.stack(wkv_s_all))
```

```python
import numpy as np
from contextlib import ExitStack
import concourse.bass as bass
import concourse.mybir as mybir
from concourse.bass_utils import run_bass_kernel_spmd

F32 = mybir.dt.float32
BF16 = mybir.dt.bfloat16
ALU = mybir.AluOpType
AF = mybir.ActivationFunctionType
AX = mybir.AxisListType

D = 2048; DFF = 5632; SEQ = 2048; NS = 16; NMEM = 256; DR = 1024; DCV = 1024
SHIFT = 3360; INC = 5408; CW = 31; NH = 16
TB = 512; NPB = SEQ // TB; CH = 64; NCH = TB // CH
ALPHA = float(2.0 ** 0.25); LN_EPS = 1e-5; GN_EPS = 64e-5
EM05 = float(np.exp(-0.5))
SEM_ROT = 20000


class Sch:
    def __init__(self, nc, es, n_dma_sems=16):
        self.nc = nc; self.es = es
        self.E = {'pe': nc.tensor, 'act': nc.scalar, 'dve': nc.vector, 'pool': nc.gpsimd, 'sp': nc.sync}
        self.semid = 0; self.cur = {}
        for e in self.E:
            self._new_sem(e)
        self.waited = {e: {} for e in self.E}
        self.pendwait = {e: False for e in self.E}
        self.res = {}
        self.pending = {e: ([], []) for e in self.E}
        self.dring = {}
        for q in ('sp', 'pool', 'act'):
            ring = []
            for i in range(n_dma_sems):
                h = es.enter_context(nc.semaphore(f"d_{q}_{i}"))
                self.semid += 1
                ring.append([h, 0, self.semid])
            self.dring[q] = [ring, 0]

    def _new_sem(self, e):
        self.semid += 1
        h = self.es.enter_context(self.nc.semaphore(f"c_{e}_{self.semid}"))
        self.cur[e] = [h, 0, self.semid]

    def _wait(self, e, ev):
        h, sid, val, src = ev
        if src == e and e == 'pe':
            return
        if self.waited[e].get(sid, 0) >= val:
            return
        if self.pendwait[e]:
            self.E[e].nop(nofuse=True)
        self.E[e].wait_ge(h, val)
        self.pendwait[e] = True
        self.waited[e][sid] = val

    def _deps(self, e, reads, writes):
        for k in reads:
            r = self.res.get(k)
            if r is not None and r['w'] is not None:
                self._wait(e, r['w'])
        for k in writes:
            r = self.res.get(k)
            if r is not None:
                if r['w'] is not None:
                    self._wait(e, r['w'])
                for ev in r['r']:
                    self._wait(e, ev)

    @staticmethod
    def _prune(evs):
        best = {}
        for ev in evs:
            if ev[1] not in best or best[ev[1]][2] < ev[2]:
                best[ev[1]] = ev
        return list(best.values())

    def _register(self, ev, reads, writes):
        for k in reads:
            r = self.res.setdefault(k, {'w': None, 'r': []})
            r['r'].append(ev)
            if len(r['r']) > 48:
                r['r'] = self._prune(r['r'])
        for k in writes:
            self.res[k] = {'w': ev, 'r': []}

    def _chk(self, e, keys):
        for k in keys:
            for e2 in self.E:
                if e2 != e and (k in self.pending[e2][1]):
                    raise RuntimeError(f"resource {k} has unsignaled pending write on {e2}")

    def op(self, e, fn, reads=(), writes=(), sig=True):
        reads = list(reads); writes = list(writes)
        if e != 'pe':
            ex = [k for k in reads if k.startswith('pb')]
            if ex:
                reads = [k for k in reads if not k.startswith('pb')]
                writes = writes + ex
        self._chk(e, reads + writes)
        self._deps(e, reads, writes)
        inst = fn()
        self.pendwait[e] = False
        pr, pw = self.pending[e]
        if not sig:
            pr.extend(reads); pw.extend(writes)
            return inst
        c = self.cur[e]
        c[1] += 1
        inst.then_inc(c[0], 1)
        ev = (c[0], c[2], c[1], e)
        self._register(ev, pr + reads, pw + writes)
        self.pending[e] = ([], [])
        if c[1] >= SEM_ROT:
            self._new_sem(e)
        return inst

    def dma(self, q, out, in_, reads=(), writes=(), **kw):
        reads = list(reads); writes = list(writes)
        self._chk(None, reads + writes)
        self._deps(q, reads, writes)
        ring, idx = self.dring[q]
        slot = ring[idx % len(ring)]
        self.dring[q][1] = idx + 1
        if slot[1] > 0:
            self._wait(q, (slot[0], slot[2], slot[1] * 16, None))
        inst = self.E[q].dma_start(out=out, in_=in_, **kw)
        self.pendwait[q] = False
        slot[1] += 1
        inst.then_inc(slot[0], 16)
        ev = (slot[0], slot[2], slot[1] * 16, None)
        self._register(ev, reads, writes)
        return ev

    def barrier(self):
        evs = []
        for q, (ring, idx) in self.dring.items():
            for slot in ring:
                if slot[1] > 0:
                    evs.append((slot[0], slot[2], slot[1] * 16, None))
        for e in self.E:
            c = self.cur[e]
            if c[1] > 0:
                evs.append((c[0], c[2], c[1], e))
        for e in self.E:
            if self.pending[e][0] or self.pending[e][1]:
                raise RuntimeError(f"pending unsignaled ops on {e}")
            for ev in evs:
                self._wait(e, ev)

    def finish(self):
        self.barrier()


class _Stop(Exception):
    pass


def build(dev=None):
    dev = dev or {}
    stopflag = [False]
    DV_MEMKV = dev.get('memkv', True); DV_BLOCKS = dev.get('blocks', list(range(NPB + 1)))
    DV_ST = dev.get('stages', ('load', 'ffn1', 'mixer', 'attn', 'ffn2', 'store'))
    nc = bass.Bass("TRN2", target_bir_lowering=False)

    def din(name, shape):
        if name in dev.get('small', ()):
            shape = [1] * len(shape)
        return nc.dram_tensor(name, list(shape), F32, kind="ExternalInput").ap()

    def dout(name, shape):
        return nc.dram_tensor(name, list(shape), F32, kind="ExternalOutput").ap()

    def dscr(name, shape):
        return nc.dram_tensor(name, list(shape), F32, kind="Internal").ap()

    x_p = din("x_p", [SEQ, D]); x_s = din("x_s", [NS, D]); mem = din("mem", [NMEM, D])
    st_shift = din("st_shift", [NS, SHIFT]); st_conv = din("st_conv", [NS, 30, DCV])
    st_wkv = din("st_wkv", [NS, NH, 64, 64]); ck = din("ck", [NS, NMEM, D]); cv = din("cv", [NS, NMEM, D])
    W = {}
    for nm, shp in [("ffn1_w1", [D, DFF]), ("ffn1_w3", [D, DFF]), ("ffn1_w2", [DFF, D]), ("w_in", [D, INC]),
                    ("w2_decay", [64, DR]), ("a2_iclr", [64, DR]), ("g2_gate", [160, DR]), ("w_out", [D, D]),
                    ("w_mq", [D, D]), ("w_mk", [D, D]), ("w_mv", [D, D]), ("w_mo", [D, D]),
                    ("ffn2_w1", [D, DFF]), ("ffn2_w3", [D, DFF]), ("ffn2_w2", [DFF, D]), ("conv_w", [CW, DCV])]:
        W[nm] = din(nm, shp)
    VEC = {}
    for nm, n in [("ln1_g", D), ("ln1_b", D), ("ln2_g", D), ("ln2_b", D), ("ln3_g", D), ("ln3_b", D),
                  ("ln4_g", D), ("ln4_b", D), ("mu_shift", SHIFT), ("w0", DR), ("a0", DR), ("k_k", DR),
                  ("k_a", DR), ("r_k", DR), ("gn_g", DR), ("gn_b", DR), ("conv_b", DCV), ("conv_ln_g", DCV),
                  ("conv_ln_b", DCV), ("beta_rwkv", DR), ("beta_conv", DCV)]:
        VEC[nm] = din(nm, [n])
    y_p = dout("y_p", [SEQ, D]); y_s = dout("y_s", [NS, D]); o_shift_p = dout("o_shift_p", [SHIFT])
    o_conv_p = dout("o_conv_p", [30, DCV]); o_wkv_p = dout("o_wkv_p", [NH, 64, 64])
    o_mk = dout("o_mk", [NMEM, D]); o_mv = dout("o_mv", [NMEM, D]); o_shift_s = dout("o_shift_s", [NS, SHIFT])
    o_conv_s = dout("o_conv_s", [NS, 30, DCV]); o_wkv_s = dout("o_wkv_s", [NS, NH, 64, 64])
    scr_q = dscr("scr_q", [NS, D]); scr_x = dscr("scr_x", [5, NS, DR]); scr_xf = dscr("scr_xf", [128, 16, TB])

    with ExitStack() as es:
        s = Sch(nc, es)
        V = nc.vector; A = nc.scalar; PE = nc.tensor; G = nc.gpsimd

        uid = [0]

        def T(name, shape, dt=F32, st=None):
            uid[0] += 1
            return (st or es).enter_context(nc.sbuf_tensor(f"{name}_{uid[0]}", list(shape), dt))

        PSA = es.enter_context(nc.psum_tensor("psa", [128, 2048], F32))
        PSB = es.enter_context(nc.psum_tensor("psb", [128, 2048], F32))

        def bank(i):
            t = PSA if i < 4 else PSB
            o = (i % 4) * 512
            return t, o, f"pb{i}"

        def dve(fn, r=(), w=(), sig=True): return s.op('dve', fn, r, w, sig)
        def act(fn, r=(), w=(), sig=True): return s.op('act', fn, r, w, sig)
        def pe(fn, r=(), w=(), sig=True): return s.op('pe', fn, r, w, sig)
        def pool(fn, r=(), w=(), sig=True): return s.op('pool', fn, r, w, sig)

        ident = T("ident", [128, 128]); ones = T("ones", [128, 128]); bones = T("bones", [128, 128])
        MUs = T("mus", [64, 64]); MUi = T("mui", [64, 64]); MLs = T("mls", [64, 64])
        MU = T("mu4", [64, 2, 4, 64]); sel = T("sel", [128, NS])
        pool(lambda: G.memset(ident[:], 1.0), w=['ident'])
        pool(lambda: G.affine_select(out=ident[:], in_=ident[:], pattern=[[-1, 128]], compare_op=ALU.is_equal,
                                     fill=0.0, base=0, channel_multiplier=1), r=['ident'], w=['ident'])
        pool(lambda: G.memset(ones[:], 1.0), w=['ones'])
        pool(lambda: G.memset(bones[:], 0.0), w=['bones'])
        pool(lambda: G.memset(bones[0:64, 0:64], 1.0), w=['bones'])
        pool(lambda: G.memset(bones[64:128, 64:128], 1.0), w=['bones'])
        for t_, op_, cm_, pat_ in ((MUs, ALU.is_gt, -1, 1), (MUi, ALU.is_ge, -1, 1), (MLs, ALU.is_gt, 1, -1)):
            pool(lambda t_=t_: G.memset(t_[:], 1.0), w=[t_.name])
            pool(lambda t_=t_, op_=op_, cm_=cm_, pat_=pat_: G.affine_select(
                out=t_[:], in_=t_[:], pattern=[[pat_, 64]], compare_op=op_, fill=0.0, base=0,
                channel_multiplier=cm_), r=[t_.name], w=[t_.name])
        for hh in range(2):
            for blk in range(4):
                src = MUs if blk in (0, 2) else MUi
                pool(lambda hh=hh, blk=blk, src=src: G.tensor_copy(out=MU[:, hh, blk, :], in_=src[:]),
                     r=[src.name], w=['mu4'])
        selh = T("selh", [128, 4, NS])
        pool(lambda: G.memset(selh[:], 1.0), w=['selh'])
        pool(lambda: G.affine_select(out=selh[:], in_=selh[:], pattern=[[-32, 4], [-1, NS]], compare_op=ALU.is_equal,
                                     fill=0.0, base=0, channel_multiplier=1), r=['selh'], w=['selh'])
        pool(lambda: G.memset(sel[:], 1.0), w=['sel'])
        pool(lambda: G.affine_select(out=sel[:], in_=sel[:], pattern=[[-8, NS]], compare_op=ALU.is_ge, fill=0.0,
                                     base=0, channel_multiplier=1), r=['sel'], w=['sel'])
        pool(lambda: G.affine_select(out=sel[:], in_=sel[:], pattern=[[8, NS]], compare_op=ALU.is_ge, fill=0.0,
                                     base=7, channel_multiplier=-1), r=['sel'], w=['sel'])

        if dev.get('upto') == 'consts':
            s.finish(); return nc
        rows = []
        col = {}
        nrow = 0
        for nm in VEC:
            n = VEC[nm].shape[0]
            col[nm] = nrow
            rows.append((nm, n))
            nrow += (n + 127) // 128
        col['conv_w'] = nrow
        nrow += CW * 8
        NPRM = nrow
        PRM = T("prm", [128, ((NPRM + 127) // 128) * 128])
        with ExitStack() as st0:
            ngrp = (NPRM + 127) // 128
            STG = [T(f"stg{g}", [128, 128], st=st0) for g in range(ngrp)]
            for g in range(ngrp):
                dve(lambda g=g: V.memset(STG[g][:], 0.0), w=[f"stg{g}"])

            def put_rows(r0, ap2d, nr, width=128):
                done = 0
                while done < nr:
                    g = (r0 + done) // 128; p = (r0 + done) % 128
                    k = min(nr - done, 128 - p)
                    s.dma('sp', STG[g][p:p + k, 0:width], ap2d[done:done + k, :], writes=[f"stg{g}"])
                    done += k
            for nm, n in rows:
                full = n // 128
                if full:
                    put_rows(col[nm], VEC[nm][0:full * 128].rearrange("(c p) -> c p", p=128), full)
                if n % 128:
                    put_rows(col[nm] + full, VEC[nm][full * 128:n].rearrange("(c p) -> c p", p=n % 128), 1, n % 128)
            put_rows(col['conv_w'], W["conv_w"].rearrange("w (j p) -> (w j) p", p=128), CW * 8)
            for g in range(ngrp):
                t, o, key = bank(4 + g % 4)
                pe(lambda g=g, t=t, o=o: PE.transpose(t[:, o:o + 128], STG[g][:], ident[:]),
                   r=[f"stg{g}", 'ident'], w=[key])
                dve(lambda g=g, t=t, o=o: V.tensor_copy(out=PRM[:, g * 128:(g + 1) * 128], in_=t[:, o:o + 128]),
                    r=[key], w=['prm'])
            s.barrier()

        if dev.get('upto') == 'params':
            s.finish(); return nc

        def pc(nm, c):
            return PRM[:, col[nm] + c: col[nm] + c + 1]

        def cwc(w, j):
            return PRM[:, col['conv_w'] + w * 8 + j: col['conv_w'] + w * 8 + j + 1]
        AG = T("ag", [128, 8 * 16])
        for i, nm in enumerate(["ln1_g", "ln1_b", "ln2_g", "ln2_b", "ln3_g", "ln3_b", "ln4_g", "ln4_b"]):
            dve(lambda i=i, nm=nm: V.tensor_scalar(out=AG[:, i * 16:(i + 1) * 16], in0=PRM[:, col[nm]:col[nm] + 16],
                                                  scalar1=ALPHA, scalar2=None, op0=ALU.mult), r=['prm'], w=['ag'])
        LW = T("lw", [128, DR]); G1 = T("g1", [128, DR]); G2 = T("g2", [128, DR])
        dve(lambda: V.memset(G2[:], 0.0), w=['g2'])
        s.dma('sp', LW[0:64, :], W["w2_decay"], writes=['lw'])
        s.dma('sp', LW[64:128, :], W["a2_iclr"], writes=['lw'])
        s.dma('sp', G1[:], W["g2_gate"][0:128, :], writes=['g1'])
        s.dma('sp', G2[0:32, :], W["g2_gate"][128:160, :], writes=['g2'])

        if dev.get('upto') == 'lora':
            s.finish(); return nc
        xf = T("xf", [128, 16, TB]); xb = T("xb", [128, 16, TB], BF16)
        NW = 4
        wbuf = [T(f"wb{i}", [128, 22, 128], BF16) for i in range(NW)]
        wctr = [0]
        KT = T("kt", [128, 16, NMEM], BF16); VB = T("vb", [128, 2, D], BF16)
        carry = T("carry", [128, 27]); M0 = T("m0", [64, 8, 2, 64]); UBC = T("ubc", [128, 8, 30])
        mean = T("mean", [128, TB]); rstd = T("rstd", [128, TB]); lnt = T("lnt", [128, TB])
        sq = [T(f"sq{i}", [128, TB]) for i in range(2)]
        dve(lambda: V.memset(carry[:], 0.0), w=['carry'])
        dve(lambda: V.memset(M0[:], 0.0), w=['m0'])
        dve(lambda: V.memset(UBC[:], 0.0), w=['ubc'])

        def XF(c): return f"xf{c}"
        def XB(c): return f"xb{c}"
        XFA = [XF(c) for c in range(16)]; XBA = [XB(c) for c in range(16)]

        def load_w(Wd, K, c0, w):
            KC = K // 128
            parts = []
            k0 = 0
            while k0 < KC:
                kn = min(22, KC - k0)
                slot = wctr[0] % NW; wctr[0] += 1
                s.dma('pool', wbuf[slot][:, 0:kn, 0:w],
                      Wd[k0 * 128:(k0 + kn) * 128, c0:c0 + w].rearrange("(k p) n -> p k n", p=128),
                      writes=[f"wb{slot}"])
                parts.append((wbuf[slot], f"wb{slot}", kn))
                k0 += kn
            return parts

        gctr = [0]

        def gbank():
            i = gctr[0] % 3; gctr[0] += 1
            return bank(i)

        def mm_group(ps_ap, pkey, parts, inT, inkeys, w, nt):
            KC = sum(p[2] for p in parts)
            k = 0
            for (wt, wkey, kn) in parts:
                for kk_ in range(kn):
                    pe(lambda: PE.matmul(ps_ap, lhsT=wt[:, kk_, 0:w], rhs=inT[:, k, 0:nt], start=(k == 0),
                                         stop=(k == KC - 1)), r=[wkey] + inkeys, w=[pkey], sig=(k == KC - 1))
                    k += 1

        class LNState:
            pass
        ln = LNState()

        def ln_start(nt, nchunks, ztile, zkeys):
            ln.nt = nt; ln.n = nchunks; ln.z = ztile; ln.zk = zkeys; ln.pend = None; ln.cnt = 0

        def ln_push(c):
            nt = ln.nt
            sl = sq[ln.cnt % 2]; sk = f"sq{ln.cnt % 2}"
            act(lambda: A.activation(out=sl[:, 0:nt], in_=ln.z[:, c, 0:nt], func=AF.Square), r=[ln.zk[c]], w=[sk])
            prev = ln.pend
            ln.pend = (c, sl, sk, ln.cnt)
            ln.cnt += 1
            if prev is not None:
                ln_emit(prev)

        def ln_emit(p):
            c, sl, sk, i = p
            nt = ln.nt
            first = (i == 0); last = (i == ln.n - 1)
            pe(lambda: PE.matmul(PSB[:, 0:nt], lhsT=ones[:], rhs=ln.z[:, c, 0:nt], start=first, stop=last),
               r=['ones', ln.zk[c]], w=['pb4'], sig=last)
            pe(lambda: PE.matmul(PSB[:, 512:512 + nt], lhsT=ones[:], rhs=sl[:, 0:nt], start=first, stop=last),
               r=['ones', sk], w=['pb5'], sig=True)

        def ln_finish(eps):
            ln_emit(ln.pend); ln.pend = None
            nt = ln.nt; inv = 1.0 / (ln.n * 128)
            act(lambda: A.mul(out=mean[:, 0:nt], in_=PSB[:, 0:nt], mul=inv), r=['pb4'], w=['mean'])
            dve(lambda: V.tensor_tensor(out=lnt[:, 0:nt], in0=mean[:, 0:nt], in1=mean[:, 0:nt], op=ALU.mult),
                r=['mean'], w=['lnt'])
            dve(lambda: V.scalar_tensor_tensor(out=rstd[:, 0:nt], in0=PSB[:, 512:512 + nt], scalar=inv,
                                               in1=lnt[:, 0:nt], op0=ALU.mult, op1=ALU.subtract),
                r=['pb5', 'lnt'], w=['rstd'])
            dve(lambda: V.tensor_scalar(out=rstd[:, 0:nt], in0=rstd[:, 0:nt], scalar1=eps, scalar2=None,
                                        op0=ALU.add), r=['rstd'], w=['rstd'])
            act(lambda: A.activation(out=rstd[:, 0:nt], in_=rstd[:, 0:nt], func=AF.Ln), r=['rstd'], w=['rstd'])
            act(lambda: A.activation(out=rstd[:, 0:nt], in_=rstd[:, 0:nt], func=AF.Exp, scale=-0.5),
                r=['rstd'], w=['rstd'])

        def ln_norm(c):
            nt = ln.nt
            dve(lambda: V.tensor_tensor(out=lnt[:, 0:nt], in0=ln.z[:, c, 0:nt], in1=mean[:, 0:nt], op=ALU.subtract),
                r=[ln.zk[c], 'mean'], w=['lnt'])
            dve(lambda: V.tensor_tensor(out=lnt[:, 0:nt], in0=lnt[:, 0:nt], in1=rstd[:, 0:nt], op=ALU.mult),
                r=['lnt', 'rstd'], w=['lnt'])

        def ln_apply_stream(li, nt, final=False):
            gname = f"ln{li + 1}_g"; bname = f"ln{li + 1}_b"
            for c in range(16):
                ln_norm(c)
                if not final:
                    act(lambda c=c: A.activation(out=xb[:, c, 0:nt], in_=lnt[:, 0:nt], func=AF.Identity,
                                                 bias=pc(bname, c), scale=pc(gname, c)), r=['lnt', 'prm'], w=[XB(c)])
                    dve(lambda c=c: V.tensor_scalar(out=xf[:, c, 0:nt], in0=lnt[:, 0:nt],
                                                    scalar1=AG[:, (2 * li) * 16 + c:(2 * li) * 16 + c + 1],
                                                    scalar2=AG[:, (2 * li + 1) * 16 + c:(2 * li + 1) * 16 + c + 1],
                                                    op0=ALU.mult, op1=ALU.add), r=['lnt', 'ag'], w=[XF(c)])
                else:
                    dve(lambda c=c: V.tensor_scalar(out=xf[:, c, 0:nt], in0=lnt[:, 0:nt], scalar1=pc(gname, c),
                                                    scalar2=pc(bname, c), op0=ALU.mult, op1=ALU.add),
                        r=['lnt', 'prm'], w=[XF(c)])

        def down_gemm(Wd, K, inT, inkeys, nt, scale):
            ln_start(nt, 16, xf, XFA)
            for c in range(16):
                parts = load_w(Wd, K, c * 128, 128)
                t, o, key = gbank()
                mm_group(t[:, o:o + nt], key, parts, inT, inkeys, 128, nt)
                dve(lambda c=c, t=t, o=o: V.scalar_tensor_tensor(out=xf[:, c, 0:nt], in0=t[:, o:o + nt], scalar=scale,
                                                                 in1=xf[:, c, 0:nt], op0=ALU.mult, op1=ALU.add),
                    r=[key, XF(c)], w=[XF(c)])
                ln_push(c)
            ln_finish(LN_EPS)

        def ffn(w1, w3, w2, li, nt, st, final=False):
            hT = T(f"hT{li}", [128, 44, TB], BF16, st=st)
            sil = [T(f"sil{li}_{i}", [128, TB], st=st) for i in range(2)]
            for f in range(44):
                pa = load_w(W[w1], D, f * 128, 128)
                pb_ = load_w(W[w3], D, f * 128, 128)
                ta, oa, ka = gbank(); tb, ob, kb = gbank()
                mm_group(ta[:, oa:oa + nt], ka, pa, xb, XBA, 128, nt)
                mm_group(tb[:, ob:ob + nt], kb, pb_, xb, XBA, 128, nt)
                sl = sil[f % 2]; sk = f"sil{f % 2}"
                act(lambda ta=ta, oa=oa, sl=sl: A.activation(out=sl[:, 0:nt], in_=ta[:, oa:oa + nt], func=AF.Silu),
                    r=[ka], w=[sk])
                dve(lambda f=f, tb=tb, ob=ob, sl=sl: V.tensor_tensor(out=hT[:, f, 0:nt], in0=sl[:, 0:nt],
                                                                    in1=tb[:, ob:ob + nt], op=ALU.mult),
                    r=[sk, kb], w=[f"hT{f}"])
            down_gemm(W[w2], DFF, hT, [f"hT{f}" for f in range(44)], nt, 0.5)
            ln_apply_stream(li, nt, final=final)

        mctr = [0]

        def mbank():
            i = 6 + (mctr[0] % 2); mctr[0] += 1
            return bank(i)

        with ExitStack() as st1:
          if DV_MEMKV:
              memt = T("memt", [128, 2, D], st=st1)
              memT = T("memT", [128, 16, NMEM], BF16, st=st1)
              ktm = T("ktm", [128, 2, D], st=st1)
              fmt = T("fmt", [128, NMEM], st=st1)
              s.dma('sp', memt[:], mem.rearrange("(t p) d -> p t d", p=128), writes=['memt'])
              for c in range(16):
                  t, o, key = mbank()
                  for tt in range(2):
                      pe(lambda c=c, tt=tt, t=t, o=o: PE.transpose(t[:, o + tt * 128:o + (tt + 1) * 128],
                                                                  memt[:, tt, c * 128:(c + 1) * 128], ident[:]),
                         r=['memt', 'ident'], w=[key], sig=(tt == 1))
                  act(lambda c=c, t=t, o=o: A.copy(out=memT[:, c, :], in_=t[:, o:o + NMEM]), r=[key], w=['memT'])
              for which, Wd, od in (("k", W["w_mk"], o_mk), ("v", W["w_mv"], o_mv)):
                  for c in range(16):
                      parts = load_w(Wd, D, c * 128, 128)
                      t, o, key = gbank()
                      mm_group(t[:, o:o + NMEM], key, parts, memT, ['memT'], 128, NMEM)
                      if which == "k":
                          act(lambda c=c, t=t, o=o: A.copy(out=KT[:, c, :], in_=t[:, o:o + NMEM]), r=[key], w=['kt'])
                      dve(lambda t=t, o=o: V.tensor_copy(out=fmt[:], in_=t[:, o:o + NMEM]), r=[key], w=['fmt'])
                      t2, o2, key2 = mbank()
                      for tt in range(2):
                          pe(lambda tt=tt, t2=t2, o2=o2: PE.transpose(t2[:, o2 + tt * 128:o2 + (tt + 1) * 128],
                                                                       fmt[:, tt * 128:(tt + 1) * 128], ident[:]),
                             r=['fmt', 'ident'], w=[key2], sig=(tt == 1))
                      dve(lambda c=c, t2=t2, o2=o2: V.tensor_copy(
                          out=ktm[:, :, c * 128:(c + 1) * 128],
                          in_=t2[:, o2:o2 + 256].rearrange("p (t n) -> p t n", n=128)), r=[key2], w=['ktm'])
                      if which == "v":
                          act(lambda c=c, t2=t2, o2=o2: A.copy(
                              out=VB[:, :, c * 128:(c + 1) * 128],
                              in_=t2[:, o2:o2 + 256].rearrange("p (t n) -> p t n", n=128)), r=[key2], w=['vb'])
                  s.dma('sp', od.rearrange("(t p) d -> p t d", p=128), ktm[:], reads=['ktm'])
              s.barrier()

        def load_x(src_ap, nt, st):
            ntile = (nt + 127) // 128
            xt = T("xt", [128, 4, D], st=st)
            for ti in range(ntile):
                n = min(128, nt - ti * 128)
                s.dma('sp', xt[0:n, ti, :], src_ap[ti * 128:ti * 128 + n, :], writes=[f"xt{ti}"])
            for c in range(dev.get('loadc', 16)):
                t, o, key = mbank()
                for ti in range(ntile):
                    n = min(128, nt - ti * 128)
                    pe(lambda c=c, ti=ti, n=n, t=t, o=o: PE.transpose(t[:, o + ti * 128:o + ti * 128 + n],
                                                                     xt[0:n, ti, c * 128:(c + 1) * 128],
                                                                     ident[0:n, 0:n]),
                       r=[f"xt{ti}", 'ident'], w=[key], sig=(ti == ntile - 1))
                act(lambda c=c, t=t, o=o: A.copy(out=xb[:, c, 0:nt], in_=t[:, o:o + nt]), r=[key], w=[XB(c)])
                dve(lambda c=c, t=t, o=o: V.tensor_scalar(out=xf[:, c, 0:nt], in0=t[:, o:o + nt], scalar1=ALPHA,
                                                          scalar2=None, op0=ALU.mult), r=[key], w=[XF(c)])

        def store_y(dst_ap, nt, st):
            ntile = (nt + 127) // 128
            yt = T("yt", [128, 4, D], st=st)
            for ti in range(ntile):
                n = min(128, nt - ti * 128)
                for q4 in range(4):
                    t, o, key = mbank()
                    for cc in range(4):
                        c = q4 * 4 + cc
                        pe(lambda c=c, cc=cc, ti=ti, n=n, t=t, o=o: PE.transpose(
                            t[0:n, o + cc * 128:o + (cc + 1) * 128], xf[:, c, ti * 128:ti * 128 + n], ident[:]),
                           r=[XF(c), 'ident'], w=[key], sig=(cc == 3))
                    if q4 % 2 == 0:
                        act(lambda q4=q4, ti=ti, n=n, t=t, o=o: A.copy(out=yt[0:n, ti, q4 * 512:(q4 + 1) * 512],
                                                                      in_=t[0:n, o:o + 512]), r=[key], w=[f"yt{ti}"])
                    else:
                        dve(lambda q4=q4, ti=ti, n=n, t=t, o=o: V.tensor_copy(out=yt[0:n, ti, q4 * 512:(q4 + 1) * 512],
                                                                             in_=t[0:n, o:o + 512]),
                            r=[key], w=[f"yt{ti}"])
                s.dma('sp', dst_ap[ti * 128:ti * 128 + n, :], yt[0:n, ti, :], reads=[f"yt{ti}"])

        def mixer(nt, blk, sample, st):
            TW = NS if sample else TB
            mixT = T("mixT", [128, 16, TW], BF16, st=st)
            MIX = [f"mix{c}" for c in range(16)]
            Pi = T("P0", [128, 1 + TW], st=st); pk = "P0"
            dtmp = T("dtmp", [128, TW], st=st)
            LWD = T("lwd", [128, TW], st=st); SG1 = T("sg1", [128, TW], st=st); SG2 = T("sg2", [128, TW], st=st)
            dve(lambda: V.memset(SG2[:], 0.0), w=['sg2'])
            if sample:
                SST = T("sst", [128, 27, NS], st=st)
                psr = [T(f"psr{i}", [NS, 128], st=st) for i in range(2)]
                with ExitStack() as s0:
                    stt = T("stt", [NS, SHIFT], st=s0)
                    s.dma('sp', stt[:], st_shift, writes=['stt'])
                    for m in range(27):
                        w = 128 if m < 26 else 32
                        t, o, key = mbank()
                        pe(lambda: PE.transpose(t[0:w, o:o + NS], stt[:, m * 128:m * 128 + w], ident[0:NS, 0:NS]),
                           r=['stt', 'ident'], w=[key])
                        dve(lambda: V.tensor_copy(out=SST[0:w, m, :], in_=t[0:w, o:o + NS]), r=[key], w=['sst'])
                    s.barrier()
            rctr = [0]

            def tok_out(src_ap, skey, w, dst_ap):
                t2_, o2_, key2 = mbank()
                pe(lambda: PE.transpose(t2_[0:NS, o2_:o2_ + w], src_ap, ident[0:w, 0:w]), r=[skey, 'ident'], w=[key2])
                i = rctr[0] % 2; rctr[0] += 1
                act(lambda: A.copy(out=psr[i][:, 0:w], in_=t2_[0:NS, o2_:o2_ + w]), r=[key2], w=[f"psr{i}"])
                s.dma('sp', dst_ap, psr[i][:, 0:w], reads=[f"psr{i}"])

            def shift_chunk(m, dst_ap, dkey):
                w = 128 if m < 26 else 32
                parts = load_w(W["w_in"], D, m * 128, w)
                t, o, key = gbank()
                mm_group(t[0:w, o:o + nt], key, parts, xb, XBA, w, nt)
                act(lambda: A.copy(out=Pi[0:w, 1:1 + nt], in_=t[0:w, o:o + nt]), r=[key], w=[pk])
                if not sample:
                    act(lambda: A.copy(out=Pi[0:w, 0:1], in_=carry[0:w, m:m + 1]), r=['carry'], w=[pk])
                    act(lambda: A.copy(out=carry[0:w, m:m + 1], in_=Pi[0:w, nt:nt + 1]), r=[pk], w=['carry'])
                    dve(lambda: V.tensor_tensor(out=dtmp[0:w, 0:nt], in0=Pi[0:w, 0:nt], in1=Pi[0:w, 1:1 + nt],
                                                op=ALU.subtract), r=[pk], w=['dtmp'])
                else:
                    dve(lambda: V.tensor_tensor(out=dtmp[0:w, 0:nt], in0=SST[0:w, m, :], in1=Pi[0:w, 1:1 + nt],
                                                op=ALU.subtract), r=[pk, 'sst'], w=['dtmp'])
                    tok_out(Pi[0:w, 1:1 + nt], pk, w, o_shift_s[:, m * 128:m * 128 + w])
                dve(lambda: V.scalar_tensor_tensor(out=dst_ap, in0=dtmp[0:w, 0:nt], scalar=pc("mu_shift", m)[0:w, :],
                                                   in1=Pi[0:w, 1:1 + nt], op0=ALU.mult, op1=ALU.add),
                    r=['dtmp', pk, 'prm'], w=[dkey])

            shift_chunk(24, LWD[:, 0:nt], 'lwd')
            act(lambda: A.activation(out=LWD[0:64, 0:nt], in_=LWD[0:64, 0:nt], func=AF.Tanh), r=['lwd'], w=['lwd'])
            shift_chunk(25, SG1[:, 0:nt], 'sg1')
            act(lambda: A.activation(out=SG1[:, 0:nt], in_=SG1[:, 0:nt], func=AF.Sigmoid), r=['sg1'], w=['sg1'])
            shift_chunk(26, SG2[0:32, 0:nt], 'sg2')
            act(lambda: A.activation(out=SG2[0:32, 0:nt], in_=SG2[0:32, 0:nt], func=AF.Sigmoid), r=['sg2'], w=['sg2'])

            if dev.get('mixupto') == 'lora':
                stopflag[0] = True
                return
            with ExitStack() as stc:
                CC = T("cc", [128, 8, nt], st=stc)
                CCK = [f"cc{j}" for j in range(8)]
                c1 = T("c1", [128, max(TW, 128)], st=stc); c2 = T("c2", [128, max(TW, 128)], st=stc)
                if not sample:
                    UB = T("ub", [128, 8, 30 + TB], st=stc)
                    dve(lambda: V.tensor_copy(out=UB[:, :, 0:30], in_=UBC[:]), r=['ubc'], w=[f"ub{j}" for j in range(8)])
                if sample:
                    CST = T("cst", [128, 4, DCV], st=stc); CWT = T("cwt", [128, 4, DCV], st=stc)
                    CPR = T("cpr", [128, DCV], st=stc); CTK = T("ctk", [NS, DCV], st=stc)
                    dve(lambda: V.memset(CST[:], 0.0), w=['cst'])
                    dve(lambda: V.memset(CWT[:], 0.0), w=['cwt'])
                    for sI in range(NS):
                        s.dma('sp', CST[sI * 8:sI * 8 + 7, :, :],
                              st_conv[sI, 0:28, :].rearrange("(g t) c -> g t c", t=4), writes=['cst'])
                        s.dma('sp', CST[sI * 8 + 7:sI * 8 + 8, 0:2, :], st_conv[sI:sI + 1, 28:30, :], writes=['cst'])
                        s.dma('sp', CWT[sI * 8:sI * 8 + 7, :, :],
                              W["conv_w"][0:28, :].rearrange("(g t) c -> g t c", t=4), writes=['cwt'])
                        s.dma('sp', CWT[sI * 8 + 7:sI * 8 + 8, 0:2, :],
                              W["conv_w"][28:30, :].rearrange("(o t) c -> o t c", o=1), writes=['cwt'])
                    dve(lambda: V.tensor_tensor(out=CST[:], in0=CST[:], in1=CWT[:], op=ALU.mult),
                        r=['cst', 'cwt'], w=['cst'])
                    dve(lambda: V.tensor_reduce(out=CPR[:], in_=CST[:].rearrange("p t c -> p c t"), axis=AX.X,
                                                op=ALU.add), r=['cst'], w=['cpr'])
                    for hb in range(2):
                        t, o, key = mbank()
                        pe(lambda: PE.matmul(t[0:NS, o:o + 512], lhsT=sel[:], rhs=CPR[:, hb * 512:(hb + 1) * 512],
                                             start=True, stop=True), r=['sel', 'cpr'], w=[key])
                        act(lambda: A.copy(out=CTK[:, hb * 512:(hb + 1) * 512], in_=t[0:NS, o:o + 512]),
                            r=[key], w=['ctk'])
                    s.dma('sp', o_conv_s[:, 0:29, :], st_conv[:, 1:30, :])
                for j in range(8):
                    pa = load_w(W["w_in"], D, SHIFT + j * 128, 128)
                    pb_ = load_w(W["w_in"], D, SHIFT + DCV + j * 128, 128)
                    ta, oa, ka = gbank(); tb, ob, kb = gbank()
                    mm_group(ta[:, oa:oa + nt], ka, pa, xb, XBA, 128, nt)
                    mm_group(tb[:, ob:ob + nt], kb, pb_, xb, XBA, 128, nt)
                    act(lambda: A.activation(out=c1[:, 0:nt], in_=tb[:, ob:ob + nt], func=AF.Sigmoid), r=[kb], w=['c1'])
                    if not sample:
                        dve(lambda: V.tensor_tensor(out=UB[:, j, 30:30 + nt], in0=c1[:, 0:nt], in1=ta[:, oa:oa + nt],
                                                    op=ALU.mult), r=['c1', ka], w=[f"ub{j}"])
                        dve(lambda: V.tensor_scalar(out=CC[:, j, 0:nt], in0=UB[:, j, 0:nt], scalar1=cwc(0, j),
                                                    scalar2=pc("conv_b", j), op0=ALU.mult, op1=ALU.add),
                            r=[f"ub{j}", 'prm'], w=[CCK[j]])
                        for wv_ in range(1, CW):
                            dve(lambda: V.scalar_tensor_tensor(out=CC[:, j, 0:nt], in0=UB[:, j, wv_:wv_ + nt],
                                                               scalar=cwc(wv_, j), in1=CC[:, j, 0:nt],
                                                               op0=ALU.mult, op1=ALU.add),
                                r=[f"ub{j}", 'prm', CCK[j]], w=[CCK[j]])
                        if blk == NPB - 1:
                            t2_, o2_, key2 = mbank()
                            pe(lambda: PE.transpose(t2_[0:30, o2_:o2_ + 128], UB[:, j, nt:nt + 30], ident[:]),
                               r=[f"ub{j}", 'ident'], w=[key2])
                            act(lambda: A.copy(out=c2[0:30, 0:128], in_=t2_[0:30, o2_:o2_ + 128]), r=[key2], w=['c2'])
                            s.dma('sp', o_conv_p[:, j * 128:(j + 1) * 128], c2[0:30, 0:128], reads=['c2'])
                        dve(lambda: V.tensor_copy(out=UBC[:, j, :], in_=UB[:, j, nt:nt + 30]), r=[f"ub{j}"], w=['ubc'])
                    else:
                        dve(lambda: V.tensor_tensor(out=c2[:, 0:nt], in0=c1[:, 0:nt], in1=ta[:, oa:oa + nt],
                                                    op=ALU.mult), r=['c1', ka], w=['c2'])
                        tok_out(c2[:, 0:nt], 'c2', 128, o_conv_s[:, 29, j * 128:(j + 1) * 128])
                        t4, o4, key4 = mbank()
                        pe(lambda: PE.transpose(t4[:, o4:o4 + NS], CTK[:, j * 128:(j + 1) * 128], ident[0:NS, 0:NS]),
                           r=['ctk', 'ident'], w=[key4])
                        dve(lambda: V.scalar_tensor_tensor(out=CC[:, j, 0:nt], in0=c2[:, 0:nt], scalar=cwc(30, j),
                                                           in1=t4[:, o4:o4 + NS], op0=ALU.mult, op1=ALU.add),
                            r=['c2', key4, 'prm'], w=[CCK[j]])
                        dve(lambda: V.tensor_scalar(out=CC[:, j, 0:nt], in0=CC[:, j, 0:nt], scalar1=pc("conv_b", j),
                                                    scalar2=None, op0=ALU.add), r=[CCK[j], 'prm'], w=[CCK[j]])
                ln_start(nt, 8, CC, CCK)
                for j in range(8):
                    ln_push(j)
                ln_finish(LN_EPS)
                for j in range(8):
                    ln_norm(j)
                    act(lambda: A.activation(out=c1[:, 0:nt], in_=lnt[:, 0:nt], func=AF.Silu,
                                             bias=pc("conv_ln_b", j), scale=pc("conv_ln_g", j)),
                        r=['lnt', 'prm'], w=['c1'])
                    dve(lambda: V.tensor_scalar(out=mixT[:, 8 + j, 0:nt], in0=c1[:, 0:nt], scalar1=pc("beta_conv", j),
                                                scalar2=None, op0=ALU.mult), r=['c1', 'prm'], w=[MIX[8 + j]])
                s.barrier()

            if dev.get('mixupto') == 'conv':
                stopflag[0] = True
                return
            with ExitStack() as str_:
                names = ["R", "K", "Vv", "Aa", "LG", "KK", "KM", "BON", "CS", "GI", "GV", "GT", "t1", "t2"]
                tl = {n: T("m_" + n, [128, TW], st=str_) for n in names}
                R = tl["R"]; K = tl["K"]; Vv = tl["Vv"]; Aa = tl["Aa"]; LG = tl["LG"]; KK = tl["KK"]; KM = tl["KM"]
                BON = tl["BON"]; CS = tl["CS"]; GI = tl["GI"]; GV = tl["GV"]; GT = tl["GT"]; t1 = tl["t1"]; t2 = tl["t2"]
                YF = K; KD = CS; NBD = LG
                kYF = 'm_K'; kKD = 'm_CS'; kNBD = 'm_LG'
                if not sample:
                    QR = T("qr", [128, 2, TB], st=str_)
                    onesT = T("onesT", [128, TB], st=str_)
                    dve(lambda: V.memset(onesT[:], 1.0), w=['onesT'])
                    s.dma('sp', scr_xf, xf[:], reads=XFA, writes=['scr_xf'])
                    s.barrier()
                    xfl = xf[0:64].rearrange("p a b -> p (a b)")
                    TOK = xfl[:, 0:3072].rearrange("p (n a b) -> p n a b", n=NCH, a=3)
                    ASB = xfl[:, 3072:7168].rearrange("p (n h a b) -> p n h a b", n=NCH, h=2, a=4)
                    Pc = xfl[:, 7168:8192].rearrange("p (n h b) -> p n h b", n=NCH, h=2)
                    PTc = T("ptc", [64, NCH, 2, 64], st=str_)[:]
                    TT = T("tt", [64, NCH, 2, 64], st=str_)
                    WSB = T("wsb", [64, 2, 64], st=str_); USB = T("usb", [64, 2, 64], st=str_)
                    YTOK = T("ytok", [64, NCH, 128], st=str_)
                    mtmp = T("mtmp", [64, 2, 64], st=str_)
                    HB = T("hb", [64, 5, TB], st=str_)
                else:
                    FM5 = T("fm5", [128, 5, 8, NS], st=str_)
                    VS = T("vs", [128, 8, NS], st=str_); YS = T("ys", [128, 8, NS], st=str_)
                    GS = T("gs", [128, 8, NS], st=str_); BONS = T("bons", [128, 8, NS], st=str_)

                def bsum(dst, src, skey, dkey, scale):
                    t, o, key = mbank()
                    pe(lambda: PE.matmul(t[:, o:o + nt], lhsT=bones[:], rhs=src[:, 0:nt], start=True, stop=True),
                       r=['bones', skey], w=[key])
                    act(lambda: A.mul(out=dst[:, 0:nt], in_=t[:, o:o + nt], mul=scale), r=[key], w=[dkey])

                def rsqrt_(ap_, key, eps_floor=None, eps_add=None):
                    if eps_floor is not None:
                        dve(lambda: V.tensor_scalar(out=ap_, in0=ap_, scalar1=eps_floor, scalar2=None, op0=ALU.max),
                            r=[key], w=[key])
                    if eps_add is not None:
                        dve(lambda: V.tensor_scalar(out=ap_, in0=ap_, scalar1=eps_add, scalar2=None, op0=ALU.add),
                            r=[key], w=[key])
                    act(lambda: A.activation(out=ap_, in_=ap_, func=AF.Ln), r=[key], w=[key])
                    act(lambda: A.activation(out=ap_, in_=ap_, func=AF.Exp, scale=-0.5), r=[key], w=[key])

                for c in range(8):
                    shift_chunk(c, R[:, 0:nt], 'm_R')
                    shift_chunk(8 + c, K[:, 0:nt], 'm_K')
                    shift_chunk(16 + c, Vv[:, 0:nt], 'm_Vv')
                    cs_ = slice(c * 128, (c + 1) * 128)
                    t, o, key = mbank()
                    pe(lambda: PE.matmul(t[:, o:o + nt], lhsT=LW[0:64, cs_], rhs=LWD[0:64, 0:nt], start=True, stop=True),
                       r=['lw', 'lwd'], w=[key])
                    act(lambda: A.activation(out=LG[:, 0:nt], in_=t[:, o:o + nt], func=AF.Sigmoid, bias=pc("w0", c),
                                             scale=1.0), r=[key, 'prm'], w=['m_LG'])
                    dve(lambda: V.tensor_scalar(out=LG[:, 0:nt], in0=LG[:, 0:nt], scalar1=-EM05, scalar2=None,
                                                op0=ALU.mult), r=['m_LG'], w=['m_LG'])
                    t, o, key = mbank()
                    pe(lambda: PE.matmul(t[:, o:o + nt], lhsT=LW[64:128, cs_], rhs=LWD[64:128, 0:nt], start=True,
                                         stop=True), r=['lw', 'lwd'], w=[key])
                    act(lambda: A.activation(out=Aa[:, 0:nt], in_=t[:, o:o + nt], func=AF.Sigmoid, bias=pc("a0", c),
                                             scale=1.0), r=[key, 'prm'], w=['m_Aa'])
                    t, o, key = mbank()
                    pe(lambda: PE.matmul(t[:, o:o + nt], lhsT=G1[:, cs_], rhs=SG1[:, 0:nt], start=True, stop=False),
                       r=['g1', 'sg1'], w=[key], sig=False)
                    pe(lambda: PE.matmul(t[:, o:o + nt], lhsT=G2[:, cs_], rhs=SG2[:, 0:nt], start=False, stop=True),
                       r=['g2', 'sg2'], w=[key])
                    act(lambda: A.copy(out=GT[:, 0:nt], in_=t[:, o:o + nt]), r=[key], w=['m_GT'])
                    dve(lambda: V.tensor_scalar(out=KK[:, 0:nt], in0=K[:, 0:nt], scalar1=pc("k_k", c), scalar2=None,
                                                op0=ALU.mult), r=['m_K', 'prm'], w=['m_KK'])
                    dve(lambda: V.tensor_tensor(out=t1[:, 0:nt], in0=KK[:, 0:nt], in1=KK[:, 0:nt], op=ALU.mult),
                        r=['m_KK'], w=['m_t1'])
                    bsum(t2, t1, 'm_t1', 'm_t2', 1.0)
                    rsqrt_(t2[:, 0:nt], 'm_t2', eps_floor=1e-24)
                    dve(lambda: V.tensor_tensor(out=KK[:, 0:nt], in0=KK[:, 0:nt], in1=t2[:, 0:nt], op=ALU.mult),
                        r=['m_KK', 'm_t2'], w=['m_KK'])
                    dve(lambda: V.tensor_scalar(out=t1[:, 0:nt], in0=Aa[:, 0:nt], scalar1=-1.0, scalar2=pc("k_a", c),
                                                op0=ALU.add, op1=ALU.mult), r=['m_Aa', 'prm'], w=['m_t1'])
                    dve(lambda: V.scalar_tensor_tensor(out=KM[:, 0:nt], in0=t1[:, 0:nt], scalar=1.0, in1=K[:, 0:nt],
                                                       op0=ALU.add, op1=ALU.mult), r=['m_t1', 'm_K'], w=['m_KM'])
                    dve(lambda: V.scalar_tensor_tensor(out=t1[:, 0:nt], in0=R[:, 0:nt], scalar=pc("r_k", c),
                                                       in1=KM[:, 0:nt], op0=ALU.mult, op1=ALU.mult),
                        r=['m_R', 'm_KM', 'prm'], w=['m_t1'])
                    bsum(t2, t1, 'm_t1', 'm_t2', 1.0)
                    dve(lambda: V.tensor_tensor(out=BON[:, 0:nt], in0=t2[:, 0:nt], in1=Vv[:, 0:nt], op=ALU.mult),
                        r=['m_t2', 'm_Vv'], w=['m_BON'])
                    if sample:
                        dve(lambda: V.tensor_copy(out=FM5[:, 0, c, :], in_=KK[:, 0:nt]), r=['m_KK'], w=['fm5'])
                        act(lambda: A.activation(out=FM5[:, 1, c, :], in_=LG[:, 0:nt], func=AF.Exp),
                            r=['m_LG'], w=['fm5'])
                        dve(lambda: V.tensor_tensor(out=FM5[:, 2, c, :], in0=KK[:, 0:nt], in1=Aa[:, 0:nt],
                                                    op=ALU.mult), r=['m_KK', 'm_Aa'], w=['fm5'])
                        dve(lambda: V.tensor_copy(out=FM5[:, 3, c, :], in_=KM[:, 0:nt]), r=['m_KM'], w=['fm5'])
                        dve(lambda: V.tensor_copy(out=FM5[:, 4, c, :], in_=R[:, 0:nt]), r=['m_R'], w=['fm5'])
                        dve(lambda: V.tensor_copy(out=VS[:, c, :], in_=Vv[:, 0:nt]), r=['m_Vv'], w=['vs'])
                        dve(lambda: V.tensor_copy(out=GS[:, c, :], in_=GT[:, 0:nt]), r=['m_GT'], w=['gs'])
                        dve(lambda: V.tensor_copy(out=BONS[:, c, :], in_=BON[:, 0:nt]), r=['m_BON'], w=['bons'])
                        for i5 in range(5):
                            tok_out(FM5[:, i5, c, :], 'fm5', 128, scr_x[i5, :, c * 128:(c + 1) * 128])
                        continue
                    if dev.get('mixupto') == 'prep':
                        stopflag[0] = True
                        return
                    dve(lambda: V.tensor_tensor_scan(out=CS[:, 0:nt], data0=onesT[:, 0:nt], data1=LG[:, 0:nt],
                                                     initial=0.0, op0=ALU.mult, op1=ALU.add),
                        r=['onesT', 'm_LG'], w=['m_CS'])
                    CS3 = CS[:, :].rearrange("p (n t) -> p n t", t=CH)
                    dve(lambda: V.tensor_tensor(out=t1[:, 0:nt], in0=CS[:, 0:nt], in1=CS[:, 0:nt], op=ALU.bypass)
                        if False else V.tensor_copy(out=t1[:, 0:nt], in_=CS[:, 0:nt]), r=['m_CS'], w=['m_t1'])
                    T13 = t1[:, :].rearrange("p (n t) -> p n t", t=CH)
                    dve(lambda: V.tensor_tensor(out=CS3[:, 1:NCH, :], in0=T13[:, 1:NCH, :],
                                                in1=T13[:, 0:NCH - 1, CH - 1:CH].to_broadcast([128, NCH - 1, CH]),
                                                op=ALU.subtract), r=['m_t1', 'm_CS'], w=['m_CS'])
                    act(lambda: A.activation(out=GI[:, 0:nt], in_=CS[:, 0:nt], func=AF.Exp), r=['m_CS'], w=['m_GI'])
                    act(lambda: A.activation(out=GV[:, 0:nt], in_=CS[:, 0:nt], func=AF.Exp, scale=-1.0),
                        r=['m_CS'], w=['m_GV'])
                    dve(lambda: V.tensor_tensor(out=t1[:, 0:nt], in0=CS[:, 0:nt], in1=LG[:, 0:nt], op=ALU.subtract),
                        r=['m_CS', 'm_LG'], w=['m_t1'])
                    act(lambda: A.activation(out=t2[:, 0:nt], in_=t1[:, 0:nt], func=AF.Exp), r=['m_t1'], w=['m_t2'])
                    dve(lambda: V.tensor_tensor(out=QR[:, 0, 0:nt], in0=KK[:, 0:nt], in1=t2[:, 0:nt], op=ALU.mult),
                        r=['m_KK', 'm_t2'], w=['qr'])
                    dve(lambda: V.tensor_tensor(out=QR[:, 1, 0:nt], in0=R[:, 0:nt], in1=GI[:, 0:nt], op=ALU.mult),
                        r=['m_R', 'm_GI'], w=['qr'])
                    dve(lambda: V.tensor_tensor(out=KD[:, 0:nt], in0=KM[:, 0:nt], in1=GV[:, 0:nt], op=ALU.mult),
                        r=['m_KM', 'm_GV', kKD], w=[kKD])
                    dve(lambda: V.tensor_tensor(out=t1[:, 0:nt], in0=KK[:, 0:nt], in1=Aa[:, 0:nt], op=ALU.mult),
                        r=['m_KK', 'm_Aa'], w=['m_t1'])
                    dve(lambda: V.scalar_tensor_tensor(out=NBD[:, 0:nt], in0=t1[:, 0:nt], scalar=-1.0,
                                                       in1=GV[:, 0:nt], op0=ALU.mult, op1=ALU.mult),
                        r=['m_t1', 'm_GV', kNBD], w=[kNBD])
                    if dev.get('mixupto') == 'scan':
                        stopflag[0] = True
                        return
                    s.dma('sp', HB[:, 0:2, :], QR[64:128, :, :], reads=['qr'], writes=['hb0'])
                    s.dma('sp', HB[:, 2, :], KD[64:128, :], reads=[kKD], writes=['hb1'])
                    s.dma('sp', HB[:, 3, :], NBD[64:128, :], reads=[kNBD], writes=['hb2'])
                    s.dma('sp', HB[:, 4, :], GI[64:128, :], reads=['m_GI'], writes=['hb3'])
                    HBK = ['hb0', 'hb1', 'hb2', 'hb3']
                    kkg = lambda hh, tk: QR[0:64, 0, tk] if hh == 0 else HB[:, 0, tk]
                    rg = lambda hh, tk: QR[0:64, 1, tk] if hh == 0 else HB[:, 1, tk]
                    qr2 = lambda hh, tk: QR[0:64, :, tk] if hh == 0 else HB[:, 0:2, tk]
                    kd_ = lambda hh, tk: KD[0:64, tk] if hh == 0 else HB[:, 2, tk]
                    nbd_ = lambda hh, tk: NBD[0:64, tk] if hh == 0 else HB[:, 3, tk]
                    gi_ = lambda hh, col: GI[0:64, col:col + 1] if hh == 0 else HB[:, 4, col:col + 1]
                    for n in range(NCH):
                        tk = slice(n * CH, (n + 1) * CH)
                        t, o, key = mbank()
                        for i3, (src, skey) in enumerate(((KD, kKD), (NBD, kNBD), (Vv, 'm_Vv'))):
                            pe(lambda: PE.transpose(t[0:64, o + i3 * 128:o + (i3 + 1) * 128], src[:, tk], ident[:]),
                               r=[skey, 'ident'], w=[key], sig=(i3 == 2))
                        act(lambda: A.copy(out=TOK[:, n, :, :],
                                           in_=t[0:64, o:o + 384].rearrange("p (a b) -> p a b", b=128)),
                            r=[key], w=['tok'])
                    if dev.get('mixupto') == 'tok':
                        stopflag[0] = True
                        return
                    for half in range(2):
                        for nn in range(4):
                            n = half * 4 + nn
                            tk = slice(n * CH, (n + 1) * CH)
                            for hh in range(2):
                                pr_ = slice(hh * 64, (hh + 1) * 64)
                                oo = nn * 512 + hh * 256
                                pe(lambda: PE.matmul(PSA[0:64, oo:oo + 128].rearrange("p (a b) -> p a b", b=64),
                                                     lhsT=kd_(hh, tk), rhs=qr2(hh, tk), start=True, stop=True),
                                   r=[kKD, 'qr'] + HBK, w=[f"pb{nn}"], sig=False)
                                pe(lambda: PE.matmul(PSA[0:64, oo + 128:oo + 256].rearrange("p (a b) -> p a b", b=64),
                                                     lhsT=nbd_(hh, tk), rhs=qr2(hh, tk), start=True, stop=True),
                                   r=[kNBD, 'qr'] + HBK, w=[f"pb{nn}"], sig=False)
                                o6 = 1024 + (n * 2 + hh) * 64
                                pe(lambda: PE.matmul(PSB[0:64, o6:o6 + 64], lhsT=kkg(hh, tk), rhs=nbd_(hh, tk),
                                                     start=True, stop=True),
                                   r=[kNBD, 'qr'] + HBK, w=['pb6', 'pb7'], sig=(hh == 1))
                            dve(lambda: V.tensor_tensor(
                                out=ASB[:, n, :, :, :],
                                in0=PSA[0:64, nn * 512:(nn + 1) * 512].rearrange("p (h a b) -> p h a b", h=2, a=4),
                                in1=MU[:], op=ALU.mult), r=[f"pb{nn}", 'mu4'], w=['asb'])
                    dve(lambda: V.tensor_tensor(
                        out=Pc, in0=PSB[0:64, 1024:2048].rearrange("p (n h b) -> p n h b", n=NCH, h=2),
                        in1=MLs[:, None, None, :].to_broadcast([64, NCH, 2, 64]), op=ALU.mult),
                        r=['pb6', 'pb7', 'mls'], w=['pc'])
                    act(lambda: A.copy(out=PTc, in_=ASB[:, :, :, 2, :]), r=['asb'], w=['ptc'])
                    dve(lambda: V.tensor_tensor(out=TT[:], in0=ASB[:, :, :, 2, :],
                                                in1=ident[0:64, None, None, 0:64].to_broadcast([64, NCH, 2, 64]),
                                                op=ALU.add), r=['asb', 'ident'], w=['tt'])
                    if dev.get('mixupto') == 'amat':
                        stopflag[0] = True
                        return
                    for lvl in range(5):
                        lastl = (lvl == 4)
                        for n in range(NCH):
                            for hh in range(2):
                                o1 = (n * 2 + hh) * 64
                                pe(lambda: PE.matmul(PSA[0:64, o1:o1 + 64], lhsT=PTc[:, n, hh, :],
                                                     rhs=Pc[:, n, hh, :], start=True, stop=True),
                                   r=['pc', 'ptc'], w=['pb0', 'pb1'], sig=(lastl and n == NCH - 1 and hh == 1))
                                if not lastl:
                                    pe(lambda: PE.matmul(PSA[0:64, 1024 + o1:1024 + o1 + 64], lhsT=Pc[:, n, hh, :],
                                                         rhs=PTc[:, n, hh, :], start=True, stop=True),
                                       r=['pc', 'ptc'], w=['pb2', 'pb3'], sig=(n == NCH - 1 and hh == 1))
                        dve(lambda: V.tensor_copy(
                            out=Pc, in_=PSA[0:64, 0:1024].rearrange("p (n h b) -> p n h b", n=NCH, h=2)),
                            r=['pb0', 'pb1'], w=['pc'])
                        if not lastl:
                            act(lambda: A.copy(
                                out=PTc, in_=PSA[0:64, 1024:2048].rearrange("p (n h b) -> p n h b", n=NCH, h=2)),
                                r=['pb2', 'pb3'], w=['ptc'])
                        for n in range(NCH):
                            for hh in range(2):
                                o1 = 1024 + (n * 2 + hh) * 64
                                pe(lambda: PE.matmul(PSB[0:64, o1:o1 + 64], lhsT=Pc[:, n, hh, :],
                                                     rhs=TT[:, n, hh, :], start=True, stop=True),
                                   r=['pc', 'tt'], w=['pb6', 'pb7'], sig=(n == NCH - 1 and hh == 1))
                        dve(lambda: V.tensor_tensor(
                            out=TT[:], in0=TT[:],
                            in1=PSB[0:64, 1024:2048].rearrange("p (n h b) -> p n h b", n=NCH, h=2), op=ALU.add),
                            r=['tt', 'pb6', 'pb7'], w=['tt'])
                    if dev.get('mixupto') == 'inv':
                        stopflag[0] = True
                        return
                    for n in range(NCH):
                        tk = slice(n * CH, (n + 1) * CH)
                        for hh in range(2):
                            pr_ = slice(hh * 64, (hh + 1) * 64)
                            hv = slice(hh * 64, (hh + 1) * 64)
                            pe(lambda: PE.matmul(PSA[0:64, hh * 64:(hh + 1) * 64], lhsT=kkg(hh, tk),
                                                 rhs=M0[:, c, hh, :], start=True, stop=False),
                               r=['qr', 'm0'] + HBK, w=['pb0'], sig=False)
                            pe(lambda: PE.matmul(PSA[0:64, hh * 64:(hh + 1) * 64], lhsT=ASB[:, n, hh, 0, :],
                                                 rhs=TOK[:, n, 2, hv], start=False, stop=True),
                               r=['asb', 'tok'], w=['pb0'], sig=(hh == 1))
                        dve(lambda: V.tensor_copy(out=WSB[:], in_=PSA[0:64, 0:128].rearrange("p (h b) -> p h b", h=2)),
                            r=['pb0'], w=['wsb'])
                        for hh in range(2):
                            pe(lambda: PE.matmul(PSA[0:64, 512 + hh * 64:512 + (hh + 1) * 64], lhsT=TT[:, n, hh, :],
                                                 rhs=WSB[:, hh, :], start=True, stop=True),
                               r=['tt', 'wsb'], w=['pb1'], sig=(hh == 1))
                        dve(lambda: V.tensor_copy(out=USB[:],
                                                  in_=PSA[0:64, 512:640].rearrange("p (h b) -> p h b", h=2)),
                            r=['pb1'], w=['usb'])
                        for hh in range(2):
                            pr_ = slice(hh * 64, (hh + 1) * 64)
                            hv = slice(hh * 64, (hh + 1) * 64)
                            os_ = 1024 + hh * 64
                            pe(lambda: PE.matmul(PSA[0:64, os_:os_ + 64], lhsT=TOK[:, n, 0, hv], rhs=TOK[:, n, 2, hv],
                                                 start=True, stop=False), r=['tok'], w=['pb2'], sig=False)
                            pe(lambda: PE.matmul(PSA[0:64, os_:os_ + 64], lhsT=TOK[:, n, 1, hv], rhs=USB[:, hh, :],
                                                 start=False, stop=True), r=['tok', 'usb'], w=['pb2'], sig=(hh == 1))
                        for hh in range(2):
                            pr_ = slice(hh * 64, (hh + 1) * 64)
                            hv = slice(hh * 64, (hh + 1) * 64)
                            oy = 1536 + hh * 64
                            pe(lambda: PE.matmul(PSA[0:64, oy:oy + 64], lhsT=rg(hh, tk), rhs=M0[:, c, hh, :],
                                                 start=True, stop=False), r=['qr', 'm0'] + HBK, w=['pb3'], sig=False)
                            pe(lambda: PE.matmul(PSA[0:64, oy:oy + 64], lhsT=ASB[:, n, hh, 1, :], rhs=TOK[:, n, 2, hv],
                                                 start=False, stop=False), r=['asb', 'tok'], w=['pb3'], sig=False)
                            pe(lambda: PE.matmul(PSA[0:64, oy:oy + 64], lhsT=ASB[:, n, hh, 3, :], rhs=USB[:, hh, :],
                                                 start=False, stop=True), r=['asb', 'usb'], w=['pb3'], sig=(hh == 1))
                        act(lambda: A.copy(out=YTOK[:, n, :], in_=PSA[0:64, 1536:1664]), r=['pb3'], w=['ytok'])
                        dve(lambda: V.tensor_tensor(out=mtmp[:],
                                                    in0=PSA[0:64, 1024:1152].rearrange("p (h b) -> p h b", h=2),
                                                    in1=M0[:, c, :, :], op=ALU.add), r=['pb2', 'm0'], w=['mtmp'])
                        for hh in range(2):
                            dve(lambda: V.tensor_scalar(out=M0[:, c, hh, :], in0=mtmp[:, hh, :],
                                                        scalar1=gi_(hh, n * CH + CH - 1), scalar2=None,
                                                        op0=ALU.mult), r=['mtmp', 'm_GI'] + HBK, w=['m0'])
                    if dev.get('mixupto') == 'chain':
                        stopflag[0] = True
                        return
                    t, o, key = mbank()
                    for n in range(NCH):
                        pe(lambda: PE.transpose(t[:, o + n * CH:o + (n + 1) * CH], YTOK[:, n, :], ident[0:64, 0:64]),
                           r=['ytok', 'ident'], w=[key], sig=(n == NCH - 1))
                    act(lambda: A.copy(out=YF[:, 0:nt], in_=t[:, o:o + nt]), r=[key], w=[kYF])
                    gn_and_gate(c, YF, kYF, BON, 'm_BON', GT, 'm_GT', nt, t1, t2, bsum, rsqrt_, mixT)
                    if dev.get('mixupto') == 'gn':
                        stopflag[0] = True
                        return

                if not sample:
                    if blk == NPB - 1:
                        t, o, key = mbank()
                        pe(lambda: PE.transpose(t[0:27, o:o + 128], carry[:, 0:27], ident[:]),
                           r=['carry', 'ident'], w=[key])
                        act(lambda: A.copy(out=t2[0:27, 0:128], in_=t[0:27, o:o + 128]), r=[key], w=['m_t2'])
                        s.dma('sp', o_shift_p[0:26 * 128].rearrange("(c p) -> c p", p=128), t2[0:26, 0:128],
                              reads=['m_t2'])
                        s.dma('sp', o_shift_p[26 * 128:SHIFT].rearrange("(c p) -> c p", p=32), t2[26:27, 0:32],
                              reads=['m_t2'])
                        for c in range(8):
                            t, o, key = mbank()
                            for hh in range(2):
                                pe(lambda: PE.transpose(t[0:64, o + hh * 64:o + (hh + 1) * 64], M0[:, c, hh, :],
                                                        ident[0:64, 0:64]), r=['m0', 'ident'], w=[key], sig=(hh == 1))
                            dst = t1 if c < 4 else GV
                            act(lambda: A.copy(out=dst[0:64, (c % 4) * 128:(c % 4 + 1) * 128], in_=t[0:64, o:o + 128]),
                                r=[key], w=['m_t1' if c < 4 else 'm_GV'])
                        s.dma('sp', o_wkv_p[0:8].rearrange("h v k -> v h k"),
                              t1[0:64, 0:512].rearrange("p (h k) -> p h k", k=64), reads=['m_t1'])
                        s.dma('sp', o_wkv_p[8:16].rearrange("h v k -> v h k"),
                              GV[0:64, 0:512].rearrange("p (h k) -> p h k", k=64), reads=['m_GV'])
                    s.barrier()
                    s.dma('sp', xf[:], scr_xf, reads=['scr_xf'], writes=XFA)
                else:
                    with ExitStack() as stu:
                        SS = T("ss", [128, 4, 8, 64], st=stu); stmp = T("stmp", [128, 4, 8, 64], st=stu)
                        BX = [T(f"bx{i}", [128, 4, 8, 64], st=stu) for i in range(5)]
                        sa = T("sa", [128, 4, 8], st=stu)
                        for g4 in range(4):
                            ssl = slice(g4 * 4, (g4 + 1) * 4)
                            for hh in range(2):
                                s.dma('sp', SS[hh * 64:(hh + 1) * 64],
                                      st_wkv[ssl].rearrange("s (c h) v k -> h v s c k", h=2)[hh], writes=['ss'])
                                for i5 in range(5):
                                    src = scr_x[i5, ssl, :].rearrange("s (c h k) -> h s c k", h=2, k=64)[hh]
                                    s.dma('sp', BX[i5][hh * 64:(hh + 1) * 64], src.partition_broadcast(64),
                                          reads=['scr_x'], writes=[f"bx{i5}"])
                            SSf = SS[:]; TMf = stmp[:]
                            dve(lambda: V.tensor_tensor(out=TMf, in0=SSf, in1=BX[0][:], op=ALU.mult),
                                r=['ss', 'bx0'], w=['stmp'])
                            dve(lambda: V.tensor_reduce(out=sa[:], in_=TMf, axis=AX.X, op=ALU.add), r=['stmp'], w=['sa'])
                            dve(lambda: V.tensor_tensor(out=SSf, in0=SSf, in1=BX[1][:], op=ALU.mult),
                                r=['ss', 'bx1'], w=['ss'])
                            dve(lambda: V.tensor_tensor(out=TMf, in0=BX[2][:],
                                                        in1=sa[:, :, :, None].to_broadcast([128, 4, 8, 64]),
                                                        op=ALU.mult), r=['bx2', 'sa'], w=['stmp'])
                            dve(lambda: V.tensor_tensor(out=SSf, in0=SSf, in1=TMf, op=ALU.subtract),
                                r=['ss', 'stmp'], w=['ss'])
                            dve(lambda: V.tensor_tensor(
                                out=TMf, in0=BX[3][:],
                                in1=VS[:, :, ssl].rearrange("p c s -> p s c")[:, :, :, None].to_broadcast([128, 4, 8, 64]),
                                op=ALU.mult), r=['bx3', 'vs'], w=['stmp'])
                            dve(lambda: V.tensor_tensor(out=SSf, in0=SSf, in1=TMf, op=ALU.add),
                                r=['ss', 'stmp'], w=['ss'])
                            dve(lambda: V.tensor_tensor(out=TMf, in0=SSf, in1=BX[4][:], op=ALU.mult),
                                r=['ss', 'bx4'], w=['stmp'])
                            dve(lambda: V.tensor_reduce(out=YS[:, :, ssl].rearrange("p c s -> p s c"), in_=TMf,
                                                        axis=AX.X, op=ALU.add), r=['stmp'], w=['ys'])
                            for hh in range(2):
                                s.dma('sp', o_wkv_s[ssl].rearrange("s (c h) v k -> h v s c k", h=2)[hh],
                                      SS[hh * 64:(hh + 1) * 64], reads=['ss'])
                        s.barrier()
                    for c in range(8):
                        dve(lambda: V.tensor_copy(out=YF[:, 0:nt], in_=YS[:, c, :]), r=['ys'], w=[kYF])
                        dve(lambda: V.tensor_copy(out=BON[:, 0:nt], in_=BONS[:, c, :]), r=['bons'], w=['m_BON'])
                        dve(lambda: V.tensor_copy(out=GT[:, 0:nt], in_=GS[:, c, :]), r=['gs'], w=['m_GT'])
                        gn_and_gate(c, YF, kYF, BON, 'm_BON', GT, 'm_GT', nt, t1, t2, bsum, rsqrt_, mixT)
                s.barrier()
            down_gemm(W["w_out"], D, mixT, MIX, nt, 1.0)
            ln_apply_stream(1, nt)

        def gn_and_gate(c, YF, ykey, BON, bkey, GT, gkey, nt, t1, t2, bsum, rsqrt_, mixT):
            bsum(t1, YF, ykey, 'm_t1', 1.0 / 64)
            dve(lambda: V.tensor_tensor(out=YF[:, 0:nt], in0=YF[:, 0:nt], in1=t1[:, 0:nt], op=ALU.subtract),
                r=[ykey, 'm_t1'], w=[ykey])
            dve(lambda: V.tensor_tensor(out=t1[:, 0:nt], in0=YF[:, 0:nt], in1=YF[:, 0:nt], op=ALU.mult),
                r=[ykey], w=['m_t1'])
            bsum(t2, t1, 'm_t1', 'm_t2', 1.0 / 64)
            rsqrt_(t2[:, 0:nt], 'm_t2', eps_add=GN_EPS)
            dve(lambda: V.tensor_tensor(out=YF[:, 0:nt], in0=YF[:, 0:nt], in1=t2[:, 0:nt], op=ALU.mult),
                r=[ykey, 'm_t2'], w=[ykey])
            dve(lambda: V.tensor_scalar(out=YF[:, 0:nt], in0=YF[:, 0:nt], scalar1=pc("gn_g", c), scalar2=pc("gn_b", c),
                                        op0=ALU.mult, op1=ALU.add), r=[ykey, 'prm'], w=[ykey])
            dve(lambda: V.tensor_tensor(out=YF[:, 0:nt], in0=YF[:, 0:nt], in1=BON[:, 0:nt], op=ALU.add),
                r=[ykey, bkey], w=[ykey])
            dve(lambda: V.scalar_tensor_tensor(out=mixT[:, c, 0:nt], in0=YF[:, 0:nt], scalar=pc("beta_rwkv", c),
                                               in1=GT[:, 0:nt], op0=ALU.mult, op1=ALU.mult),
                r=[ykey, gkey, 'prm'], w=[f"mix{c}"])

        def attention(nt, sample, st):
            TW = NS if sample else TB
            qT = None if sample else T("qT", [128, 16, TB], BF16, st=st)
            oT = T("oT", [128, 16, TW], BF16, st=st)
            OTK = [f"oT{c}" for c in range(16)]
            qf = T("qf", [128, 16, NS], st=st) if sample else None
            for c in range(16):
                parts = load_w(W["w_mq"], D, c * 128, 128)
                t, o, key = gbank()
                mm_group(t[:, o:o + nt], key, parts, xb, XBA, 128, nt)
                if not sample:
                    act(lambda c=c, t=t, o=o: A.mul(out=qT[:, c, 0:nt], in_=t[:, o:o + nt], mul=float(512 ** -0.5)),
                        r=[key], w=['qT'])
                else:
                    act(lambda c=c, t=t, o=o: A.mul(out=qf[:, c, :], in_=t[:, o:o + nt], mul=float(512 ** -0.5)),
                        r=[key], w=['qf'])
            if not sample:
                esb = T("esb", [128, NMEM], st=st); prT = T("prT", [128, 2, 128], BF16, st=st)
                mx = T("mx", [128, 4], st=st)
                for ti in range(nt // 128):
                    tk = slice(ti * 128, (ti + 1) * 128)
                    for h in range(4):
                        t, o, key = mbank()
                        for dc in range(4):
                            pe(lambda dc=dc: PE.matmul(t[:, o:o + NMEM], lhsT=qT[:, h * 4 + dc, tk], rhs=KT[:, h * 4 + dc, :],
                                                       start=(dc == 0), stop=(dc == 3)), r=['qT', 'kt'], w=[key],
                               sig=(dc == 3))
                        if dev.get('attsc'):
                            dve(lambda: V.tensor_copy(out=xf[:, h * 4 + ti, 0:NMEM], in_=t[:, o:o + NMEM]),
                                r=[key], w=[XF(h * 4 + ti)])
                        dve(lambda: V.tensor_reduce(out=mx[:, 0:1], in_=t[:, o:o + NMEM], axis=AX.X, op=ALU.max),
                            r=[key], w=['mx'])
                        dve(lambda: V.tensor_scalar(out=mx[:, 1:2], in0=mx[:, 0:1], scalar1=-1.0, scalar2=None,
                                                    op0=ALU.mult), r=['mx'], w=['mx'])
                        if dev.get('attsc'):
                            dve(lambda: V.tensor_copy(out=xf[:, h * 4 + ti, 256:257], in_=mx[:, 0:1]),
                                r=['mx'], w=[XF(h * 4 + ti)])
                        if dev.get('attnomax'):
                            dve(lambda: V.memset(mx[:, 1:2], 0.0), w=['mx'])
                        act(lambda: A.activation(out=esb[:], in_=t[:, o:o + NMEM], func=AF.Exp, bias=mx[:, 1:2],
                                                 scale=1.0), r=[key, 'mx'], w=['esb'])
                        dve(lambda: V.tensor_reduce(out=mx[:, 2:3], in_=esb[:], axis=AX.X, op=ALU.add),
                            r=['esb'], w=['mx'])
                        dve(lambda: V.reciprocal(out=mx[:, 3:4], in_=mx[:, 2:3]), r=['mx'], w=['mx'])
                        dve(lambda: V.tensor_scalar(out=esb[:], in0=esb[:], scalar1=mx[:, 3:4], scalar2=None,
                                                    op0=ALU.mult), r=['esb', 'mx'], w=['esb'])
                        if dev.get('attes'):
                            dve(lambda: V.tensor_copy(out=xf[:, h * 4 + ti, 0:NMEM], in_=esb[:]),
                                r=['esb'], w=[XF(h * 4 + ti)])
                            dve(lambda: V.tensor_copy(out=xf[:, h * 4 + ti, 256:260], in_=mx[:, 0:4]),
                                r=['mx'], w=[XF(h * 4 + ti)])
                        if dev.get('attuni'):
                            dve(lambda: V.memset(esb[:], 1.0 / NMEM), w=['esb'])
                        t2_, o2_, key2 = mbank()
                        for stt_ in range(2):
                            pe(lambda stt_=stt_: PE.transpose(t2_[:, o2_ + stt_ * 128:o2_ + (stt_ + 1) * 128],
                                                              esb[:, stt_ * 128:(stt_ + 1) * 128], ident[:]),
                               r=['esb', 'ident'], w=[key2], sig=(stt_ == 1))
                        act(lambda: A.copy(out=prT[:], in_=t2_[:, o2_:o2_ + 256].rearrange("p (a b) -> p a b", b=128)),
                            r=[key2], w=['prT'])
                        t3, o3, key3 = mbank()
                        for dc in range(4):
                            for stt_ in range(2):
                                pe(lambda dc=dc, stt_=stt_: PE.matmul(
                                    t3[:, o3 + dc * 128:o3 + (dc + 1) * 128],
                                    lhsT=VB[:, stt_, h * 512 + dc * 128:h * 512 + (dc + 1) * 128], rhs=prT[:, stt_, :],
                                    start=(stt_ == 0), stop=(stt_ == 1)), r=['vb', 'prT'], w=[key3],
                                   sig=(dc == 3 and stt_ == 1))
                        dve(lambda: V.tensor_copy(out=oT[:, h * 4:(h + 1) * 4, tk],
                                                  in_=t3[:, o3:o3 + 512].rearrange("p (a b) -> p a b", b=128)),
                            r=[key3], w=OTK[h * 4:(h + 1) * 4])
            else:
                with ExitStack() as sq_:
                    qtok = T("qtok", [NS, D], st=sq_)
                    for c in range(16):
                        t, o, key = mbank()
                        pe(lambda: PE.transpose(t[0:NS, o:o + 128], qf[:, c, :], ident[:]), r=['qf', 'ident'], w=[key])
                        act(lambda: A.copy(out=qtok[:, c * 128:(c + 1) * 128], in_=t[0:NS, o:o + 128]), r=[key], w=['qtok'])
                    s.dma('sp', scr_q, qtok[:], reads=['qtok'], writes=['scr_q'])
                    s.barrier()
                KS = [T(f"ks{i}", [128, 2, D], st=st) for i in range(2)]
                QB = [T(f"qb{i}", [128, D], st=st) for i in range(1)] * 2
                scT = T("scT", [128, 2, NS, 4], st=st)
                for b in range(NS):
                    i2 = b % 2
                    s.dma('sp', KS[i2][:], ck[b].rearrange("(t p) d -> p t d", p=128), writes=[f"ks{i2}"])
                    s.dma('sp', QB[i2][:], scr_q[b:b + 1, :].partition_broadcast(128), reads=['scr_q'],
                          writes=["qb0"])
                    for stt_ in range(2):
                        dve(lambda: V.tensor_tensor(out=KS[i2][:, stt_, :], in0=KS[i2][:, stt_, :], in1=QB[i2][:],
                                                    op=ALU.mult), r=[f"ks{i2}", "qb0"], w=[f"ks{i2}"])
                        dve(lambda: V.tensor_reduce(out=scT[:, stt_, b, :],
                                                    in_=KS[i2][:, stt_, :].rearrange("p (h d) -> p h d", h=4),
                                                    axis=AX.X, op=ALU.add), r=[f"ks{i2}"], w=['scT'])
                sm = T("sm", [64, NMEM], st=st); mx = T("mxs", [64, 4], st=st)
                t, o, key = mbank()
                for stt_ in range(2):
                    pe(lambda stt_=stt_: PE.transpose(t[0:64, o + stt_ * 128:o + (stt_ + 1) * 128],
                                                      scT[:, stt_, :, :].rearrange("p b h -> p (b h)"), ident[:]),
                       r=['scT', 'ident'], w=[key], sig=(stt_ == 1))
                dve(lambda: V.tensor_reduce(out=mx[:, 0:1], in_=t[0:64, o:o + NMEM], axis=AX.X, op=ALU.max),
                    r=[key], w=['mxs'])
                dve(lambda: V.tensor_scalar(out=mx[:, 1:2], in0=mx[:, 0:1], scalar1=-1.0, scalar2=None, op0=ALU.mult),
                    r=['mxs'], w=['mxs'])
                act(lambda: A.activation(out=sm[:], in_=t[0:64, o:o + NMEM], func=AF.Exp, bias=mx[:, 1:2], scale=1.0),
                    r=[key, 'mxs'], w=['sm'])
                dve(lambda: V.tensor_reduce(out=mx[:, 2:3], in_=sm[:], axis=AX.X, op=ALU.add), r=['sm'], w=['mxs'])
                dve(lambda: V.reciprocal(out=mx[:, 3:4], in_=mx[:, 2:3]), r=['mxs'], w=['mxs'])
                dve(lambda: V.tensor_scalar(out=sm[:], in0=sm[:], scalar1=mx[:, 3:4], scalar2=None, op0=ALU.mult),
                    r=['sm', 'mxs'], w=['sm'])
                prTs = T("prTs", [128, 2, 64], st=st)
                t, o, key = mbank()
                for stt_ in range(2):
                    pe(lambda stt_=stt_: PE.transpose(t[:, o + stt_ * 64:o + (stt_ + 1) * 64],
                                                      sm[:, stt_ * 128:(stt_ + 1) * 128], ident[0:64, 0:64]),
                       r=['sm', 'ident'], w=[key], sig=(stt_ == 1))
                dve(lambda: V.tensor_copy(out=prTs[:], in_=t[:, o:o + 128].rearrange("p (a b) -> p a b", b=64)),
                    r=[key], w=['prTs'])
                Z = T("Z", [128, 2, NS, 128], st=st)
                dve(lambda: V.memset(Z[:], 0.0), w=['Z'])
                for stt_ in range(2):
                    for b in range(NS):
                        for h in range(4):
                            dve(lambda: V.tensor_copy(out=Z[:, stt_, b, h * 32 + b:h * 32 + b + 1],
                                                      in_=prTs[:, stt_, b * 4 + h:b * 4 + h + 1]),
                                r=['prTs'], w=['Z'], sig=(h == 3))
                VS_ = KS
                for b in range(NS):
                    i2 = b % 2
                    s.dma('sp', VS_[i2][:], cv[b].rearrange("(t p) d -> p t d", p=128), writes=[f"ks{i2}"])
                    for stt_ in range(2):
                        for nb in range(4):
                            pe(lambda: PE.matmul(PSA[:, nb * 512:(nb + 1) * 512], lhsT=Z[:, stt_, b, :],
                                                 rhs=VS_[i2][:, stt_, nb * 512:(nb + 1) * 512],
                                                 start=(b == 0 and stt_ == 0), stop=(b == NS - 1 and stt_ == 1)),
                               r=['Z', f"ks{i2}"], w=[f"pb{nb}"], sig=(nb == 3))
                accs = QB[0]
                for nb in range(4):
                    eng = dve if nb % 2 == 0 else act
                    if nb % 2 == 0:
                        dve(lambda: V.tensor_copy(out=accs[:, nb * 512:(nb + 1) * 512], in_=PSA[:, nb * 512:(nb + 1) * 512]),
                            r=[f"pb{nb}"], w=['qb0'])
                    else:
                        act(lambda: A.copy(out=accs[:, nb * 512:(nb + 1) * 512], in_=PSA[:, nb * 512:(nb + 1) * 512]),
                            r=[f"pb{nb}"], w=['qb0'])
                otok = T("otok", [NS, D], st=st)
                for h in range(4):
                    t, o, key = mbank()
                    pe(lambda: PE.matmul(t[0:NS, o:o + 512], lhsT=selh[:, h, :], rhs=accs[:, h * 512:(h + 1) * 512],
                                         start=True, stop=True), r=['selh', 'qb0'], w=[key])
                    act(lambda: A.copy(out=otok[:, h * 512:(h + 1) * 512], in_=t[0:NS, o:o + 512]), r=[key], w=['otok'])
                for c in range(16):
                    t, o, key = mbank()
                    pe(lambda: PE.transpose(t[:, o:o + NS], otok[:, c * 128:(c + 1) * 128], ident[0:NS, 0:NS]),
                       r=['otok', 'ident'], w=[key])
                    act(lambda: A.copy(out=oT[:, c, 0:NS], in_=t[:, o:o + NS]), r=[key], w=[OTK[c]])
            if dev.get('attsc') or dev.get('attes'):
                return
            if dev.get('attraw'):
                for c in range(16):
                    dve(lambda: V.memset(xf[:, c, :], 0.0), w=[XF(c)])
            down_gemm(W["w_mo"], D, oT, OTK, nt, 1.0)
            if not dev.get('attraw'):
                ln_apply_stream(2, nt)

        for blk in DV_BLOCKS:
          try:
              sample = (blk == NPB)
              nt = NS if sample else TB
              if 'load' in DV_ST:
                  with ExitStack() as stx:
                      load_x(x_s if sample else x_p[blk * TB:(blk + 1) * TB, :], nt, stx)
                      s.barrier()
              if 'ffn1' in DV_ST:
                  with ExitStack() as stf:
                      ffn("ffn1_w1", "ffn1_w3", "ffn1_w2", 0, nt, stf)
                      s.barrier()
              if 'mixer' in DV_ST:
                  with ExitStack() as stm:
                      mixer(nt, blk, sample, stm)
                      s.barrier()
                  if stopflag[0]:
                      break
              if 'attn' in DV_ST:
                  with ExitStack() as sta:
                      attention(nt, sample, sta)
                      s.barrier()
              if 'ffn2' in DV_ST:
                  with ExitStack() as stf2:
                      ffn("ffn2_w1", "ffn2_w3", "ffn2_w2", 3, nt, stf2, final=True)
                      s.barrier()
              if 'store' in DV_ST:
                  with ExitStack() as sty:
                      store_y(y_s if sample else y_p[blk * TB:(blk + 1) * TB, :], nt, sty)
                      s.barrier()
          except _Stop:
            break
        s.finish()
    return nc


_NC = None


def make_in_maps(inp):
    f = lambda a: np.ascontiguousarray(np.asarray(a, dtype=np.float32))
    shared = {}
    for nm in ["ffn1_w1", "ffn1_w3", "ffn1_w2", "w_in", "w2_decay", "a2_iclr", "g2_gate", "w_out", "w_mq", "w_mk",
               "w_mv", "w_mo", "ffn2_w1", "ffn2_w3", "ffn2_w2"]:
        shared[nm] = f(inp[nm][0])
    shared["conv_w"] = f(inp["conv_w"][0, :, 0, :])
    for nm in ["ln1_g", "ln1_b", "ln2_g", "ln2_b", "ln3_g", "ln3_b", "ln4_g", "ln4_b", "mu_shift", "w0", "a0", "k_k",
               "k_a", "gn_g", "gn_b", "conv_b", "conv_ln_g", "conv_ln_b", "beta_rwkv", "beta_conv"]:
        shared[nm] = f(inp[nm][0])
    shared["r_k"] = f(inp["r_k"][0].reshape(-1))
    in_maps = []
    for c in range(8):
        b = c % 4
        sl = slice(c * NS, (c + 1) * NS)
        m = dict(shared)
        m["x_p"] = f(inp["x_prompt"][b]); m["x_s"] = f(inp["x_sample"][sl, 0, :]); m["mem"] = f(inp["mem_prompt"][b])
        m["st_shift"] = f(inp["state_shift"][0, sl]); m["st_conv"] = f(inp["state_conv"][0, sl])
        m["st_wkv"] = f(inp["state_wkv"][0, sl])
        m["ck"] = f(inp["cache_mem_k"][0, sl].reshape(NS, NMEM, D)); m["cv"] = f(inp["cache_mem_v"][0, sl].reshape(NS, NMEM, D))
        in_maps.append(m)
    return in_maps


def kernel(**inp):
    global _NC
    if _NC is None:
        _NC = build()
    in_maps = make_in_maps(inp)
    res = run_bass_kernel_spmd(_NC, in_maps, core_ids=list(range(8)))
    R = res.results
    B = 4
    y_prompt = np.stack([R[b]["y_p"] for b in range(B)])
    y_sample = np.concatenate([R[c]["y_s"] for c in range(8)])[:, None, :]
    shift_p = np.stack([R[b]["o_shift_p"] for b in range(B)])[None]
    conv_p = np.stack([R[b]["o_conv_p"] for b in range(B)])[None]
    wkv_p = np.stack([R[b]["o_wkv_p"] for b in range(B)])[None]
    mk_p = np.stack([R[b]["o_mk"].reshape(NMEM, 4, 512) for b in range(B)])[None]
    mv_p = np.stack([R[b]["o_mv"].reshape(NMEM, 4, 512) for b in range(B)])[None]
    shift_s = np.concatenate([R[c]["o_shift_s"] for c in range(8)])[None]
    conv_s = np.concatenate([R[c]["o_conv_s"] for c in range(8)])[None]
    wkv_s = np.concatenate([R[c]["o_wkv_s"] for c in range(8)])[None]
    return tuple(np.ascontiguousarray(a, dtype=np.float32) for a in
                 (y_prompt, y_sample, shift_p, conv_p, wkv_p, mk_p, mv_p, shift_s, conv_s, wkv_s))
```
